# Optimizing a Trainium2 kernel written in Bass

```python
import jax, jax.numpy as jnp
from jax import lax
import numpy as np

D_MODEL = 1024
BATCH = 8
SEQ = 8192
DEPTH = 1

N_MEM = 256
D_MIX = D_MODEL
ML_HEADS = 4
ML_DH = D_MIX // 2 // ML_HEADS
ML_W = ML_HEADS * ML_DH
ML_CHUNK = 64
CONV_W = 4
FX_HEADS = 8
FX_DH = (D_MIX - ML_W) // FX_HEADS
FX_W = FX_HEADS * FX_DH
Q_BLOCK = 128
X_HEADS = 4
X_DH = D_MODEL // X_HEADS
D_FF = 4 * D_MODEL
EPS = 1e-6
IN_SIZES = (2 * ML_W, ML_W, ML_W, ML_HEADS, ML_HEADS, FX_W, FX_W, FX_W, FX_HEADS)
IN_COLS = 4 * ML_W + 2 * ML_HEADS + 3 * FX_W + FX_HEADS

kernel_name = "hymba_mlstm_fox_hybrid"


def rmsnorm(x, g):
    xf = x.astype(jnp.float32)
    y = xf * lax.rsqrt(jnp.mean(xf * xf, axis=-1, keepdims=True) + EPS)
    return (y * g.astype(jnp.float32)).astype(x.dtype)


def causal_conv(u, w, b):
    S = u.shape[1]
    up = jnp.pad(u, ((0, 0), (CONV_W - 1, 0), (0, 0)))
    out = b
    for j in range(CONV_W):
        out = out + up[:, j:j + S] * w[j]
    return out


def mlstm_chunkwise(q, k, v, ig, lf):
    B, S, H, dh = q.shape
    nc = S // ML_CHUNK
    k = k * (dh ** -0.5)

    def to_chunks(t):
        t = t.reshape((B, nc, ML_CHUNK) + t.shape[2:])
        return jnp.moveaxis(t, (1, 3), (0, 2))

    causal = jnp.tril(jnp.ones((ML_CHUNK, ML_CHUNK), dtype=bool))

    def body(carry, xs):
        C, n, m = carry
        qc, kc, vc, ic, fc = xs
        b = jnp.cumsum(fc, axis=-1)
        g = b[..., -1]
        dmat = b[..., :, None] - b[..., None, :] + ic[..., None, :]
        dmat = jnp.where(causal, dmat, -jnp.inf)
        m_inter = b + m[..., None]
        m_t = jnp.maximum(m_inter, jnp.max(dmat, axis=-1))
        scores = jnp.einsum('bhtd,bhsd->bhts', qc, kc) * jnp.exp(dmat - m_t[..., None])
        inter = jnp.exp(m_inter - m_t)
        num = (jnp.einsum('bhts,bhsd->bhtd', scores, vc)
               + inter[..., None] * jnp.einsum('bhed,bhtd->bhte', C, qc))
        den = scores.sum(-1) + inter * jnp.einsum('bhd,bhtd->bht', n, qc)
        h = num / jnp.maximum(jnp.abs(den), jnp.exp(-m_t))[..., None]
        a = g[..., None] - b + ic
        m_new = jnp.maximum(g + m, jnp.max(a, axis=-1))
        decay = jnp.exp(g + m - m_new)
        wa = jnp.exp(a - m_new[..., None])
        C_new = decay[..., None, None] * C + jnp.einsum('bhs,bhse,bhsd->bhed', wa, vc, kc)
        n_new = decay[..., None] * n + jnp.einsum('bhs,bhsd->bhd', wa, kc)
        return (C_new, n_new, m_new), h

    init = (jnp.zeros((B, H, dh, dh), jnp.float32), jnp.zeros((B, H, dh), jnp.float32),
            jnp.zeros((B, H), jnp.float32))
    _, hs = lax.scan(body, init, (to_chunks(q), to_chunks(k), to_chunks(v), to_chunks(ig), to_chunks(lf)))
    hs = jnp.moveaxis(hs, (0, 2), (1, 3))
    return hs.reshape(B, S, H, dh)


def forgetting_attention(q, k, v, lf):
    B, S, H, dh = q.shape
    nb = S // Q_BLOCK
    cT = jnp.cumsum(lf, axis=1).transpose(0, 2, 1)
    qb = q.reshape(B, nb, Q_BLOCK, H, dh).transpose(1, 0, 2, 3, 4)
    cb = cT.reshape(B, H, nb, Q_BLOCK).transpose(2, 0, 1, 3)
    kpos = jnp.arange(S)
    scale = dh ** -0.5

    def block(args):
        qi, ci, blk = args
        qpos = blk * Q_BLOCK + jnp.arange(Q_BLOCK)
        s = jnp.einsum('bqhd,bkhd->bhqk', qi, k, preferred_element_type=jnp.float32) * scale
        s = s + (ci[..., :, None] - cT[..., None, :])
        s = jnp.where(qpos[:, None] >= kpos[None, :], s, -jnp.inf)
        p = jax.nn.softmax(s, axis=-1)
        return jnp.einsum('bhqk,bkhd->bqhd', p.astype(v.dtype), v)

    out = lax.map(block, (qb, cb, jnp.arange(nb)))
    return out.transpose(1, 0, 2, 3, 4).reshape(B, S, H, dh)


def cross_attention(h, memn, w_q, w_kv, w_o):
    B, S, D = h.shape
    M = memn.shape[1]
    q = (h @ w_q).reshape(B, S, X_HEADS, X_DH)
    k, v = jnp.split(memn @ w_kv, 2, axis=-1)
    k = k.reshape(B, M, X_HEADS, X_DH)
    v = v.reshape(B, M, X_HEADS, X_DH)
    s = jnp.einsum('bqhd,bmhd->bhqm', q, k, preferred_element_type=jnp.float32) * (X_DH ** -0.5)
    p = jax.nn.softmax(s, axis=-1)
    o = jnp.einsum('bhqm,bmhd->bqhd', p.astype(v.dtype), v).reshape(B, S, D)
    return o @ w_o


def setup_inputs(seed: int = 0) -> dict:
    key = jax.random.key(seed)
    ks = jax.random.split(key, 24)
    f32 = jnp.float32

    def w(k, shape, fan_in):
        return jax.random.normal(k, shape, f32) * (fan_in ** -0.5)

    def gain(k, shape):
        return 1.0 + 0.1 * jax.random.normal(k, shape, f32)

    def small(k, shape):
        return 0.01 * jax.random.normal(k, shape, f32)

    return {
        "x": jax.random.normal(ks[0], (BATCH, SEQ, D_MODEL), f32),
        "mem": jax.random.normal(ks[1], (BATCH, N_MEM, D_MODEL), f32),
        "ln1": gain(ks[2], (DEPTH, D_MODEL)),
        "w_in": w(ks[3], (DEPTH, D_MODEL, IN_COLS), D_MODEL),
        "ml_conv_w": w(ks[4], (DEPTH, CONV_W, 2 * ML_W), CONV_W),
        "ml_conv_b": small(ks[5], (DEPTH, 2 * ML_W)),
        "ml_b_i": small(ks[6], (DEPTH, ML_HEADS)),
        "ml_b_f": 3.0 + 0.1 * jax.random.normal(ks[7], (DEPTH, ML_HEADS), f32),
        "ml_norm": gain(ks[8], (DEPTH, ML_W)),
        "fx_b_f": 3.0 + 0.1 * jax.random.normal(ks[9], (DEPTH, FX_HEADS), f32),
        "w_out": w(ks[10], (DEPTH, D_MIX, D_MODEL), D_MIX),
        "ln_x": gain(ks[11], (DEPTH, D_MODEL)),
        "ln_mem": gain(ks[12], (DEPTH, D_MODEL)),
        "w_xq": w(ks[13], (DEPTH, D_MODEL, D_MODEL), D_MODEL),
        "w_xkv": w(ks[14], (DEPTH, D_MODEL, 2 * D_MODEL), D_MODEL),
        "w_xo": w(ks[15], (DEPTH, D_MODEL, D_MODEL), D_MODEL),
        "ln2": gain(ks[16], (DEPTH, D_MODEL)),
        "w_ff1": w(ks[17], (DEPTH, D_MODEL, D_FF), D_MODEL),
        "w_ff2": w(ks[18], (DEPTH, D_FF, D_MODEL), D_FF),
        "ln_f": gain(ks[19], (D_MODEL,)),
    }


def reference(x, mem, ln1, w_in, ml_conv_w, ml_conv_b, ml_b_i, ml_b_f, ml_norm, fx_b_f, w_out,
              ln_x, ln_mem, w_xq, w_xkv, w_xo, ln2, w_ff1, w_ff2, ln_f):
    B, S, _ = x.shape
    splits = [int(v) for v in np.cumsum(IN_SIZES)[:-1]]
    for l in range(DEPTH):
        h = rmsnorm(x, ln1[l])
        z = h @ w_in[l]
        ml_qk, ml_v, ml_o, ml_i, ml_f, fx_q, fx_k, fx_v, fx_f = jnp.split(z, splits, axis=-1)

        ml_qk = jax.nn.silu(causal_conv(ml_qk, ml_conv_w[l], ml_conv_b[l]))
        mq, mk = jnp.split(ml_qk.astype(jnp.float32), 2, axis=-1)
        mq = mq.reshape(B, S, ML_HEADS, ML_DH)
        mk = mk.reshape(B, S, ML_HEADS, ML_DH)
        mv = ml_v.astype(jnp.float32).reshape(B, S, ML_HEADS, ML_DH)
        ig = ml_i.astype(jnp.float32) + ml_b_i[l].astype(jnp.float32)
        lf = jax.nn.log_sigmoid(ml_f.astype(jnp.float32) + ml_b_f[l].astype(jnp.float32))
        mh = mlstm_chunkwise(mq, mk, mv, ig, lf)
        mh = mh * lax.rsqrt(jnp.mean(mh * mh, axis=-1, keepdims=True) + EPS)
        mh = mh * ml_norm[l].astype(jnp.float32).reshape(ML_HEADS, ML_DH)
        ml_out = (mh.reshape(B, S, ML_W) * jax.nn.sigmoid(ml_o.astype(jnp.float32))).astype(x.dtype)

        fq = fx_q.reshape(B, S, FX_HEADS, FX_DH)
        fk = fx_k.reshape(B, S, FX_HEADS, FX_DH)
        fv = fx_v.reshape(B, S, FX_HEADS, FX_DH)
        flf = jax.nn.log_sigmoid(fx_f.astype(jnp.float32) + fx_b_f[l].astype(jnp.float32))
        fx_out = forgetting_attention(fq, fk, fv, flf).reshape(B, S, FX_W)

        x = x + jnp.concatenate([ml_out, fx_out], axis=-1) @ w_out[l]

        x = x + cross_attention(rmsnorm(x, ln_x[l]), rmsnorm(mem, ln_mem[l]), w_xq[l], w_xkv[l], w_xo[l])

        u = jax.nn.relu(rmsnorm(x, ln2[l]) @ w_ff1[l])
        x = x + (u * u) @ w_ff2[l]
    return rmsnorm(x, ln_f)
```

```python
from contextlib import ExitStack
import numpy as np
import concourse.bass as bass
import concourse.mybir as mybir
from concourse.bass_utils import run_bass_kernel_spmd

F32 = mybir.dt.float32
BF16 = mybir.dt.bfloat16
AF = mybir.ActivationFunctionType
ALU = mybir.AluOpType

ENGS = ["pe", "act", "dve", "pool", "sp"]
N_DMA_SEMS = 8

P = 128
D = 1024
SEQ = 8192
T = 512
NB = T // P
NMEM = 256
DFF = 4096
INC = 3600
EPS = 1e-6
NW = 5
WB_ELEMS = 4160
KCH = 16
LN_ALPHA = float(-0.5 * np.log(128.0))


class Sched:
    def __init__(self):
        self.streams = {e: [] for e in ENGS}
        self.known = {e: {} for e in ENGS}
        self.cnt = {}
        self.opinfo = []
        self.last_w = {}
        self.readers = {}
        self.dma_rr = {e: 0 for e in ENGS}

    def add(self, eng, emit, reads=(), writes=(), dma=False):
        idx = len(self.opinfo)
        bank_toks = [t for t in reads if isinstance(t, tuple) and t[0] == "bank"]
        if bank_toks:
            reads = [t for t in reads if not (isinstance(t, tuple) and t[0] == "bank")]
            writes = list(writes) + bank_toks
        deps = set()
        for t in reads:
            w = self.last_w.get(t)
            if w is not None:
                deps.add(w)
        for t in writes:
            w = self.last_w.get(t)
            if w is not None:
                deps.add(w)
            for r in self.readers.get(t, ()):
                deps.add(r)
        for t in reads:
            self.readers.setdefault(t, []).append(idx)
        for t in writes:
            self.last_w[t] = idx
            self.readers[t] = []
        known = self.known[eng]
        waits = []
        for d in sorted(deps, reverse=True):
            sem, val, clock = self.opinfo[d]
            if known.get(sem, 0) >= val:
                continue
            waits.append((sem, val))
            for k, v in clock.items():
                if known.get(k, 0) < v:
                    known[k] = v
        if dma:
            sem = "dma_%s_%d" % (eng, self.dma_rr[eng])
            self.dma_rr[eng] = (self.dma_rr[eng] + 1) % N_DMA_SEMS
            prev = self.cnt.get(sem, 0)
            if known.get(sem, 0) < prev:
                waits.append((sem, prev))
                known[sem] = prev
            val = prev + 16
            inc = 16
        else:
            sem = eng
            val = self.cnt.get(sem, 0) + 1
            inc = 1
        self.cnt[sem] = val
        clock = dict(known)
        clock[sem] = val
        self.opinfo.append((sem, val, clock))
        self.streams[eng].append((waits, emit, sem, inc))
        return idx

    def emit_all(self, block, sems):
        def run(engname):
            def body(eng):
                for waits, emit, sem, inc in self.streams[engname]:
                    for (s, v) in waits:
                        eng.wait_ge(sems[s], v)
                    inst = emit(eng)
                    inst.then_inc(sems[sem], inc)
                if engname == "sp":
                    for s, v in self.cnt.items():
                        eng.wait_ge(sems[s], v)
            return body
        block.tensor(run("pe"))
        block.scalar(run("act"))
        block.vector(run("dve"))
        block.gpsimd(run("pool"))
        block.sync(run("sp"))


def build_nc(n_tiles=SEQ // T, dbg=False):
    nc = bass.Bass("TRN2", target_bir_lowering=False)
    S_LEN = n_tiles * T

    def din(name, shape):
        return nc.dram_tensor(name, shape, F32, kind="ExternalInput").ap()

    x_d = din("x", [S_LEN, D])
    mem_d = din("mem", [NMEM, D])
    ln1_d = din("ln1", [D])
    w_in_d = din("w_in", [D, INC])
    convw_d = din("ml_conv_w", [4, D])
    convb_d = din("ml_conv_b", [D])
    bi_d = din("ml_b_i", [4])
    bf_d = din("ml_b_f", [4])
    mlnorm_d = din("ml_norm", [512])
    fxbf_d = din("fx_b_f", [8])
    w_out_d = din("w_out", [D, D])
    lnx_d = din("ln_x", [D])
    lnmem_d = din("ln_mem", [D])
    w_xq_d = din("w_xq", [D, D])
    w_xkv_d = din("w_xkv", [D, 2 * D])
    w_xo_d = din("w_xo", [D, D])
    ln2_d = din("ln2", [D])
    w_ff1_d = din("w_ff1", [D, DFF])
    w_ff2_d = din("w_ff2", [DFF, D])
    lnf_d = din("ln_f", [D])
    cst_d = din("cst", [P, 512])
    out_d = nc.dram_tensor("out", [S_LEN, D], F32, kind="ExternalOutput").ap()

    def dscr(name, shape, dt=BF16):
        return nc.dram_tensor(name, shape, dt, kind="Internal").ap()

    wb_in = dscr("wb_in", [D, INC])
    wb_out = dscr("wb_out", [D, D])
    wb_xq = dscr("wb_xq", [D, D])
    wb_xkv = dscr("wb_xkv", [D, 2 * D])
    wb_xo = dscr("wb_xo", [D, D])
    wb_ff1 = dscr("wb_ff1", [D, DFF])
    wb_ff2 = dscr("wb_ff2", [DFF, D])
    kT_d = dscr("kT_d", [4, P, S_LEN])
    v_d = dscr("v_d", [4, S_LEN, 256])

    S = Sched()
    es = ExitStack()
    import os
    STAGE = int(os.environ.get('K_STAGE', '99'))

    class StopBuild(Exception):
        pass

    def ckpt(n):
        if STAGE <= n:
            raise StopBuild()
    with es:
        def sb(name, shape, dt):
            return es.enter_context(nc.sbuf_tensor(name, shape, dt))

        xt = [sb("xt%d" % i, [P, NB, D], F32) for i in range(2)]
        hT = sb("hT", [P, 8, T], BF16)
        tok = sb("tok", [P, NB, D], BF16)
        junk = sb("junk", [P, D], BF16)
        fT2 = sb("fT2", [P, 8, T], BF16)
        stage = [sb("stage%d" % i, [P, T + 3], F32) for i in range(2)]
        acc = [sb("acc%d" % i, [P, T], F32) for i in range(2)]
        halo = sb("halo", [P, 8, 3], F32)
        mv_aug = sb("mv_aug", [P, NB, 4, 129], BF16)
        og = sb("og", [P, NB, 512], F32)
        fqz = sb("fqz", [P, 4, 2, T], BF16)
        kT_cur = sb("kT_cur", [P, 4, T], BF16)
        v_cur = sb("v_cur", [P, NB, 8, P], BF16)
        PT2 = [sb("PT%d" % i, [P, 2, T], BF16) for i in range(3)]
        PT = [PT2[0][:, 0, :], PT2[0][:, 1, :], PT2[1][:, 0, :], PT2[1][:, 1, :]]
        vst = sb("vst", [P, 8, P], BF16)
        Gs = sb("Gs", [P, 2, 8], F32)
        nonlocal_state = {"u": 0}
        osb = [sb("osb%d" % i, [P, T], F32) for i in range(2)]
        bc = [sb("bc%d" % i, [P, T], F32) for i in range(2)]
        uT = [sb("uT%d" % i, [P, 4, T], BF16) for i in range(2)]
        rl = [sb("rl%d" % i, [P, T], BF16) for i in range(2)]
        wbuf = [sb("wbuf%d" % i, [P, WB_ELEMS], BF16) for i in range(NW)]
        memKT = sb("memKT", [P, 8, NMEM], BF16)
        memV = sb("memV", [P, 2, D], BF16)
        cs = sb("cs", [P, 512], F32)
        identb = sb("identb", [P, P], BF16)
        negmb = sb("negmb", [P, P], BF16)
        onesb = sb("onesb", [P, P], BF16)
        lnf_rep = sb("lnf_rep", [P, D], F32)
        mln_rep = sb("mln_rep", [P, 512], F32)
        gcol = sb("gcol", [P, 4, 8], F32)
        cw = sb("cw", [P, 4, 8], F32)
        cb = sb("cb", [P, 8], F32)
        gbias = sb("gbias", [P, 16], F32)
        wg = sb("wg", [P, 8, 16], BF16)
        negc = sb("negc", [P, 8, SEQ // P], F32)
        bias_i = sb("bias_i", [P, 8, SEQ // P], F32)
        spaccs = sb("spaccs", [P, NB + 1, 8], F32)
        crefn = sb("crefn", [P, 8], F32)
        Cst = sb("Cst", [P, 4, 129], F32)
        Cb = sb("Cb", [P, 4, 129], BF16)
        ssq = sb("ssq", [P, NB], F32)
        ssq2 = sb("ssq2", [P, NB], F32)
        rstd2 = sb("rstd2", [P, NB], F32)
        rstd = sb("rstd", [P, NB], F32)
        gs = sb("gs", [P, NB, 16], F32)
        sp_ = sb("sp_", [P, NB, 12], F32)
        gsm = sb("gsm", [P, NB, 24], F32)
        es_ = sb("es_", [P, NB, 4], F32)
        eb_ = sb("eb_", [P, NB, 4], F32)
        eg_ = sb("eg_", [P, NB, 4], F32)
        ktok = sb("ktok", [P, 2, 4, P], BF16)
        Sm = sb("Sm", [P, 2, 4, P], BF16)
        vw = sb("vw", [P, NB, 4, 129], BF16)
        tri4 = sb("tri4", [P, 4, P], F32)
        nd = sb("nd", [P, 4, 129], F32)
        sm1 = sb("sm1", [P, 4, 4], F32)
        t1 = sb("t1", [P, 4, P], BF16)
        memx = xt[1]

        psum = es.enter_context(nc.psum_tensor("psum", [P, 8, 512], F32))
        psb = [psum[:, i, :] for i in range(8)]
        ptrs = [psum[:, 6, :].bitcast(BF16), psum[:, 7, :].bitcast(BF16)]
        psb_b = [psum[:, i, :].bitcast(BF16) for i in range(8)]
        pm = psum[:, 7, :]

        gen_rr = [0]

        def gen_bank():
            b = gen_rr[0]
            gen_rr[0] = (gen_rr[0] + 1) % 4
            return b

        wb_rr = [0]

        def next_wbuf():
            i = wb_rr[0]
            wb_rr[0] = (wb_rr[0] + 1) % NW
            return i

        def load_w(src_ap, shape3, deps=()):
            i = next_wbuf()
            n = shape3[1] * shape3[2]
            view = wbuf[i][:, 0:n].rearrange("p (a b) -> p a b", a=shape3[1])
            S.add("sp", lambda e, v=view, s=src_ap: e.dma_start(out=v, in_=s), reads=list(deps),
                  writes=[("wbuf", i)], dma=True)
            return i, view

        def wpiece(wd, c0, ncols=512):
            return wd.rearrange("(c p) n -> p c n", p=P)[:, :, c0:c0 + ncols]

        try:
            def cast_w(src, dst, rows, name):
                for r0 in range(0, rows, 256):
                    S.add("pool", lambda e, r0=r0: e.dma_start(out=dst[r0:r0 + 256, :], in_=src[r0:r0 + 256, :]),
                          writes=[(name, r0)], dma=True)
            WTOK = {}

            def wtoks(name, rows):
                return [(name, r0) for r0 in range(0, rows, 256)]
            S.add("pool", lambda e: e.memset(halo[:], 0.0), writes=["halo"])
            S.add("pool", lambda e: e.memset(spaccs[:], 0.0), writes=[("spaccs", b) for b in range(NB + 1)])
            S.add("pool", lambda e: e.memset(Cst[:], 0.0), writes=[("Cst", h) for h in range(4)])
            S.add("pool", lambda e: e.memset(Cb[:], 0.0), writes=[("Cb", h) for h in range(4)])
            S.add("pool", lambda e: e.memset(mv_aug[:], 1.0), writes=[("mv_aug", b) for b in range(NB)])
            S.add("pool", lambda e: e.memset(v_cur[:], 0.0), writes=[("v_cur", b) for b in range(NB)])
            v_cur5 = v_cur[:].rearrange("p b (pp hh) c -> p b pp hh c", hh=2)
            S.add("pool", lambda e: e.memset(v_cur5[:, :, :, 0, 64:65], 1.0), writes=[("v_cur", b) for b in range(NB)])
            S.add("pool", lambda e: e.memset(v_cur5[:, :, :, 1, 0:1], 1.0), writes=[("v_cur", b) for b in range(NB)])
            S.add("pool", lambda e: e.memset(fqz[:], 0.0), writes=[("fqz", c) for c in range(4)])

            cast_w(w_xkv_d, wb_xkv, D, "wb_xkv")
            cast_w(w_in_d, wb_in, D, "wb_in")
            cast_w(w_out_d, wb_out, D, "wb_out")
            cast_w(w_xq_d, wb_xq, D, "wb_xq")
            cast_w(w_xo_d, wb_xo, D, "wb_xo")
            cast_w(w_ff1_d, wb_ff1, D, "wb_ff1")
            cast_w(w_ff2_d, wb_ff2, DFF, "wb_ff2")

            def ld(dst, src, tokname, slow=False):
                if slow:
                    S.add("sp", lambda e: e.dma_start(out=dst, in_=src, allow_slow_non_contiguous=True),
                          writes=[tokname], dma=True)
                else:
                    S.add("sp", lambda e: e.dma_start(out=dst, in_=src), writes=[tokname], dma=True)

            ld(cs[:], cst_d, "cs")
            for i, g in enumerate([ln1_d, lnx_d, ln2_d, lnmem_d]):
                ld(gcol[:, i, :], g.rearrange("(c p) -> p c", p=P), ("gcol", i), slow=True)
            for j in range(4):
                ld(cw[:, j, :], convw_d[j, :].rearrange("(c p) -> p c", p=P), ("cw", j), slow=True)
            ld(cb[:], convb_d.rearrange("(c p) -> p c", p=P), "cb", slow=True)
            ld(gbias[:, 0:4], bi_d.partition_broadcast(P), ("gbias", 0))
            ld(gbias[:, 4:8], bf_d.partition_broadcast(P), ("gbias", 1))
            ld(gbias[:, 8:16], fxbf_d.partition_broadcast(P), ("gbias", 2))
            ld(lnf_rep[:], lnf_d.partition_broadcast(P), "lnf_rep")
            ld(mln_rep[:], mlnorm_d.partition_broadcast(P), "mln_rep")
            S.add("sp", lambda e: e.dma_start(out=memx[:, 0:2, :], in_=mem_d.rearrange("(b p) d -> p b d", p=P)), writes=[("xt", 1, 0), ("xt", 1, 1)], dma=True)
            S.add("sp", lambda e: e.dma_start(out=wg[:, :, 0:8], in_=wpiece(wb_in, 2048, 8),
                                             allow_slow_non_contiguous=True),
                  reads=wtoks("wb_in", D), writes=[("wg", 0)], dma=True)
            S.add("sp", lambda e: e.dma_start(out=wg[:, :, 8:16], in_=wpiece(wb_in, 3592, 8),
                                             allow_slow_non_contiguous=True),
                  reads=wtoks("wb_in", D), writes=[("wg", 1)], dma=True)
            W_IN = wtoks("wb_in", D)
            GB = [("gbias", 0), ("gbias", 1), ("gbias", 2)]
            WG = [("wg", 0), ("wg", 1)]

            S.add("dve", lambda e: e.tensor_copy(out=identb[:], in_=cs[:, 0:128]), reads=["cs"], writes=["identb"])
            S.add("dve", lambda e: e.tensor_copy(out=onesb[:], in_=cs[:, 256:384]), reads=["cs"], writes=["onesb"])
            S.add("dve", lambda e: e.tensor_copy(out=negmb[:], in_=cs[:, 384:512]), reads=["cs"], writes=["negmb"])
            tri_f = cs[:, 128:256]
            for h in range(4):
                S.add("dve", lambda e, h=h: e.tensor_copy(out=tri4[:, h, :], in_=cs[:, 128:256]), reads=["cs"], writes=["tri4"])
            ones_f = cs[:, 256:384]
            def rms_to_hT(src, nblk, gi, src_toks, ntok, dstT, dst_tok):
                for b in range(nblk):
                    S.add("act", lambda e, b=b: e.activation(out=junk[:], in_=src[:, b, :], func=AF.Square,
                                                              accum_out=ssq[:, b:b + 1]),
                          reads=[src_toks[b]], writes=["junk", ("ssq", b)])
                    S.add("act", lambda e, b=b: e.activation(out=rstd[:, b:b + 1], in_=ssq[:, b:b + 1], func=AF.Ln,
                                                              scale=1.0 / D, bias=EPS),
                          reads=[("ssq", b)], writes=[("rstd", b)])
                    S.add("act", lambda e, b=b: e.activation(out=rstd[:, b:b + 1], in_=rstd[:, b:b + 1], func=AF.Exp, scale=-0.5),
                          reads=[("rstd", b)], writes=[("rstd", b)])
                    S.add("dve", lambda e, b=b: e.tensor_scalar(
                        out=tok[:, b, :], in0=src[:, b, :], scalar1=rstd[:, b:b + 1], scalar2=None, op0=ALU.mult),
                        reads=[src_toks[b], ("rstd", b)], writes=[("tok", b)])
                for c in range(8):
                    half = c % 2

                    def tr(e, c=c, half=half):
                        for b in range(nblk):
                            i = e.transpose(out=ptrs[half][:, b * P:(b + 1) * P],
                                            in_=tok[:, b, c * P:(c + 1) * P], identity=identb[:])
                        return i
                    S.add("pe", tr, reads=[("tok", b) for b in range(nblk)] + ["identb"], writes=[("bank", 6 + half)])
                    eng = "act" if c % 2 == 0 else "dve"
                    if eng == "act":
                        S.add("act", lambda e, c=c, half=half: e.activation(
                            out=dstT[:, c, 0:ntok], in_=ptrs[half][:, 0:ntok], func=AF.Copy,
                            scale=gcol[:, gi, c:c + 1]),
                            reads=[("bank", 6 + half), ("gcol", gi)], writes=[(dst_tok, c)])
                    else:
                        S.add("dve", lambda e, c=c, half=half: e.tensor_scalar(
                            out=dstT[:, c, 0:ntok], in0=ptrs[half][:, 0:ntok],
                            scalar1=gcol[:, gi, c:c + 1], scalar2=None, op0=ALU.mult),
                            reads=[("bank", 6 + half), ("gcol", gi)], writes=[(dst_tok, c)])

            def mm_feat(wview, wslot, cc, actT, act_toks, ntok, bank):
                def f(e):
                    for k in range(8):
                        i = e.matmul(psb[bank][:, 0:ntok], lhsT=wview[:, k, cc * P:(cc + 1) * P], rhs=actT[:, k, 0:ntok],
                                     start=(k == 0), stop=(k == 7))
                    return i
                S.add("pe", f, reads=[("wbuf", wslot)] + act_toks, writes=[("bank", bank)])

            def mm_tok(wview, wslot, actT, act_toks, b, bank, ncols=512, c0=0, extra_reads=()):
                def f(e):
                    for k in range(8):
                        i = e.matmul(psb[bank][:, 0:ncols], lhsT=actT[:, k, b * P:(b + 1) * P],
                                     rhs=wview[:, k, c0:c0 + ncols], start=(k == 0), stop=(k == 7))
                    return i
                S.add("pe", f, reads=[("wbuf", wslot)] + act_toks + list(extra_reads), writes=[("bank", bank)])

            HT = [("hT", c) for c in range(8)]

            ckpt(0)
            rms_to_hT(memx, 2, 3, [("xt", 1, 0), ("xt", 1, 1)], NMEM, hT, "hT")
            def load_w_dep(src_ap, shape3, dep_toks):
                i = next_wbuf()
                n = shape3[1] * shape3[2]
                view = wbuf[i][:, 0:n].rearrange("p (a b) -> p a b", a=shape3[1])
                S.add("sp", lambda e, v=view, s=src_ap: e.dma_start(out=v, in_=s), reads=dep_toks,
                      writes=[("wbuf", i)], dma=True)
                return i, view

            for piece in range(2):
                slot, wv = load_w_dep(wpiece(wb_xkv, piece * 512), (P, 8, 512), wtoks("wb_xkv", D))
                for cc in range(4):
                    bank = gen_bank()
                    mm_feat(wv, slot, cc, hT, HT, NMEM, bank)
                    c = piece * 4 + cc
                    S.add("act", lambda e, c=c, bank=bank: e.activation(out=memKT[:, c, :], in_=psb[bank][:, 0:NMEM],
                                                                         func=AF.Copy),
                          reads=[("bank", bank)], writes=[("memKT", c)])
            for piece in range(2):
                slot, wv = load_w_dep(wpiece(wb_xkv, D + piece * 512), (P, 8, 512), wtoks("wb_xkv", D))
                for mb in range(2):
                    bank = gen_bank()
                    mm_tok(wv, slot, hT, HT, mb, bank)
                    S.add("dve", lambda e, mb=mb, piece=piece, bank=bank: e.tensor_copy(
                        out=memV[:, mb, piece * 512:(piece + 1) * 512], in_=psb[bank][:, :]),
                        reads=[("bank", bank)], writes=[("memV", mb, piece)])
            MEMK = [("memKT", c) for c in range(8)]
            MEMV = [("memV", mb, pc) for mb in range(2) for pc in range(2)]

            ckpt(1)
            def load_x(i):
                xb = xt[i % 2]
                S.add("sp", lambda e: e.dma_start(out=xb[:], in_=x_d[i * T:(i + 1) * T, :].rearrange("(b p) d -> p b d", p=P)),
                      writes=[("xt", i % 2, b) for b in range(NB)], dma=True)

            pending_final = []
            load_x(0)
            for ti in range(n_tiles):
                X = xt[ti % 2]
                XT = [("xt", ti % 2, b) for b in range(NB)]
                t0 = ti * T

                rms_to_hT(X, NB, 0, XT, T, hT, "hT")
                while pending_final:
                    pending_final.pop(0)()

                ckpt(2)
                for b in range(NB):
                    gb = 4 + b

                    def fg(e, b=b, gb=gb):
                        for k in range(8):
                            i = e.matmul(psb[gb][:, 0:16], lhsT=hT[:, k, b * P:(b + 1) * P],
                                         rhs=wg[:, k, :], start=(k == 0), stop=(k == 7))
                        return i
                    S.add("pe", fg, reads=HT + WG, writes=[("bank", gb)])
                    S.add("dve", lambda e, b=b, gb=gb: e.tensor_tensor(out=gs[:, b, :], in0=psb[gb][:, 0:16],
                                                                        in1=gbias[:], op=ALU.add),
                          reads=[("bank", gb)] + GB, writes=[("gs", b)])
                    S.add("act", lambda e, b=b: e.activation(out=sp_[:, b, :], in_=gs[:, b, 4:16], func=AF.Exp, scale=-1.0),
                          reads=[("gs", b)], writes=[("sp", b)])
                    S.add("act", lambda e, b=b: e.activation(out=sp_[:, b, :], in_=sp_[:, b, :], func=AF.Ln, bias=1.0),
                          reads=[("sp", b)], writes=[("sp", b)])

                for b in range(NB):
                    S.add("dve", lambda e, b=b: e.tensor_tensor(out=spaccs[:, b + 1, :], in0=spaccs[:, b, :], in1=sp_[:, b, 4:12],
                                                                 op=ALU.add),
                          reads=[("spaccs", b), ("sp", b)], writes=[("spaccs", b + 1)])
                for piece in range(2):
                    slot, wv = load_w(wpiece(wb_in, piece * 512), (P, 8, 512), W_IN)
                    for cc in range(4):
                        c = piece * 4 + cc
                        bank = gen_bank()
                        mm_feat(wv, slot, cc, hT, HT, T, bank)
                        st = stage[c % 2]
                        ac = acc[c % 2]
                        stn = ("stage", c % 2)
                        acn = ("acc", c % 2)
                        S.add("pool", lambda e, c=c, st=st: e.tensor_copy(out=st[:, 0:3], in_=halo[:, c, :]),
                              reads=["halo"], writes=[stn])
                        S.add("act", lambda e, st=st, bank=bank: e.activation(out=st[:, 3:T + 3], in_=psb[bank][:, :],
                                                                               func=AF.Copy),
                              reads=[("bank", bank)], writes=[stn])
                        S.add("pool", lambda e, c=c, st=st: e.tensor_copy(out=halo[:, c, :], in_=st[:, T:T + 3]),
                              reads=[stn], writes=["halo"])
                        S.add("act", lambda e, c=c, ac=ac, bank=bank: e.activation(
                            out=ac[:], in_=psb[bank][:, :], func=AF.Identity, scale=cw[:, 3, c:c + 1], bias=cb[:, c:c + 1]),
                            reads=[("bank", bank), ("cw", 3), "cb"], writes=[acn])
                        for j in (2, 1, 0):
                            S.add("dve", lambda e, c=c, st=st, ac=ac, j=j: e.scalar_tensor_tensor(
                                out=ac[:], in0=st[:, j:T + j], scalar=cw[:, j, c:c + 1], in1=ac[:],
                                op0=ALU.mult, op1=ALU.add),
                                reads=[stn, ("cw", j), acn], writes=[acn])
                        S.add("act", lambda e, c=c, ac=ac: e.activation(out=fT2[:, c, :], in_=ac[:], func=AF.Silu),
                              reads=[acn], writes=[("fT2", c)])
                ckpt(4)
                for b in range(NB):
                    gb = 4 + b

                    def fcs(e, b=b, gb=gb):
                        e.matmul(psb[gb][:, 16:28], lhsT=tri_f, rhs=sp_[:, b, :], start=True, stop=True)
                        e.matmul(psb[gb][:, 28:32], lhsT=ones_f, rhs=sp_[:, b, 0:4], start=True, stop=True)
                        e.matmul(psb[gb][:, 32:40], lhsT=tri_f, rhs=sp_[:, b, 4:12], start=True, stop=False)
                        i = e.matmul(psb[gb][:, 32:40], lhsT=ones_f, rhs=spaccs[:, b, :], start=False, stop=True)
                        if b == 1:
                            i = e.matmul(psb[gb][:, 40:48], lhsT=ones_f, rhs=spaccs[:, 2, :], start=True, stop=True)
                        return i
                    S.add("pe", fcs, reads=["cs", ("sp", b), ("spaccs", b)] + ([("spaccs", 2)] if b == 1 else []),
                          writes=[("bank", gb)])
                    S.add("dve", lambda e, b=b, gb=gb: e.tensor_copy(out=gsm[:, b, :], in_=psb[gb][:, 16:40]),
                          reads=[("bank", gb)], writes=[("gsm", b)])
                    if b == 1:
                        S.add("dve", lambda e, gb=gb: e.tensor_copy(out=crefn[:], in_=psb[gb][:, 40:48]),
                              reads=[("bank", gb)], writes=["crefn"])
                    blk = ti * NB + b
                    S.add("dve", lambda e, blk=blk, b=b: e.tensor_copy(out=negc[:, :, blk], in_=gsm[:, b, 16:24]),
                          reads=[("gsm", b)], writes=["negc"])
                    if b % 2 == 1:
                        S.add("dve", lambda e, b=b: e.tensor_tensor(out=Gs[:, b // 2, :], in0=gsm[:, b, 16:24], in1=gsm[:, b - 1, 16:24],
                                                                     op=ALU.subtract),
                              reads=[("gsm", b), ("gsm", b - 1)], writes=[("Gs", b // 2)])
                        S.add("act", lambda e, b=b: e.activation(out=Gs[:, b // 2, :], in_=Gs[:, b // 2, :], func=AF.Exp),
                              reads=[("Gs", b // 2)], writes=[("Gs", b // 2)])
                    S.add("dve", lambda e, b=b: e.tensor_tensor(out=es_[:, b, :], in0=gsm[:, b, 0:4], in1=gs[:, b, 0:4],
                                                                 op=ALU.add),
                          reads=[("gsm", b), ("gs", b)], writes=[("es", b)])
                    S.add("act", lambda e, b=b: e.activation(out=es_[:, b, :], in_=es_[:, b, :], func=AF.Exp,
                                                              bias=LN_ALPHA),
                          reads=[("es", b)], writes=[("es", b)])
                    S.add("act", lambda e, b=b: e.activation(out=eb_[:, b, :], in_=gsm[:, b, 0:4], func=AF.Exp, scale=-1.0),
                          reads=[("gsm", b)], writes=[("eb", b)])
                    S.add("act", lambda e, b=b: e.activation(out=eg_[:, b, :], in_=gsm[:, b, 12:16], func=AF.Exp, scale=-1.0),
                          reads=[("gsm", b)], writes=[("eg", b)])
                nblk_tot = (ti + 1) * NB
                for h in range(8):
                    S.add("dve", lambda e, h=h, nblk_tot=nblk_tot: e.tensor_scalar(out=bias_i[:, h, 0:nblk_tot], in0=negc[:, h, 0:nblk_tot],
                                                                 scalar1=crefn[:, h:h + 1], scalar2=None, op0=ALU.subtract),
                          reads=["negc", "crefn"], writes=[("bias_i", h)])
                S.add("dve", lambda e: e.tensor_copy(out=spaccs[:, 0, :], in_=spaccs[:, NB, :]),
                      reads=[("spaccs", NB)], writes=[("spaccs", 0)])
                slot, wv = load_w(wpiece(wb_in, 1024), (P, 8, 512), W_IN)
                for b in range(NB):
                    bank = gen_bank()
                    mm_tok(wv, slot, hT, HT, b, bank)
                    S.add("dve", lambda e, b=b, bank=bank: e.tensor_copy(
                        out=mv_aug[:, b, :, 0:128], in_=psb[bank][:, :].rearrange("p (h d) -> p h d", h=4)),
                        reads=[("bank", bank)], writes=[("mv_aug", b)])
                slot, wv = load_w(wpiece(wb_in, 1536), (P, 8, 512), W_IN)
                for b in range(NB):
                    bank = gen_bank()
                    mm_tok(wv, slot, hT, HT, b, bank)
                    S.add("act", lambda e, b=b, bank=bank: e.activation(out=og[:, b, :], in_=psb[bank][:, :], func=AF.Sigmoid),
                          reads=[("bank", bank)], writes=[("og", b)])
                    S.add("pool", lambda e, b=b: e.tensor_tensor(out=og[:, b, :], in0=og[:, b, :], in1=mln_rep[:], op=ALU.mult),
                          reads=[("og", b), "mln_rep"], writes=[("og", b)])
                slot, wv = load_w(wpiece(wb_in, 2056), (P, 8, 512), W_IN)
                for cc in range(4):
                    bank = gen_bank()
                    mm_feat(wv, slot, cc, hT, HT, T, bank)
                    S.add("dve", lambda e, cc=cc, bank=bank: e.tensor_copy(out=fqz[0:64, cc, 0, :], in_=psb[bank][0:64, :]),
                          reads=[("bank", bank)], writes=[("fqz", cc)])
                    S.add("dve", lambda e, cc=cc, bank=bank: e.tensor_copy(out=fqz[64:128, cc, 1, :], in_=psb[bank][64:128, :]),
                          reads=[("bank", bank)], writes=[("fqz", cc)])
                slot, wv = load_w(wpiece(wb_in, 2568), (P, 8, 512), W_IN)
                for cc in range(4):
                    bank = gen_bank()
                    mm_feat(wv, slot, cc, hT, HT, T, bank)
                    S.add("act", lambda e, cc=cc, bank=bank: e.activation(out=kT_cur[:, cc, :], in_=psb[bank][:, :], func=AF.Copy),
                          reads=[("bank", bank)], writes=[("kT_cur", cc)])
                    if ti < n_tiles - 1:
                        S.add("pool", lambda e, cc=cc, t0=t0: e.dma_start(out=kT_d[cc, :, t0:t0 + T], in_=kT_cur[:, cc, :]),
                              reads=[("kT_cur", cc)], writes=[("kT_d", cc, ti)], dma=True)
                slot, wv = load_w(wpiece(wb_in, 3080), (P, 8, 512), W_IN)
                for b in range(NB):
                    bank = gen_bank()
                    mm_tok(wv, slot, hT, HT, b, bank)
                    for hh in range(2):
                        S.add("dve", lambda e, b=b, bank=bank, hh=hh: e.tensor_copy(
                            out=v_cur5[:, b, :, hh, hh * 64:hh * 64 + 64],
                            in_=psb[bank][:, :].rearrange("p (pp hh d) -> p pp hh d", pp=4, hh=2)[:, :, hh, :]),
                            reads=[("bank", bank)], writes=[("v_cur", b)])
                    if ti < n_tiles - 1:
                        if b % 2 == 1:
                            for h in range(8):
                                S.add("dve", lambda e, b=b, h=h: e.tensor_scalar(out=vst[:, h, :], in0=v_cur[:, b, h, :],
                                                                                  scalar1=Gs[:, b // 2, h:h + 1], scalar2=None, op0=ALU.mult),
                                      reads=[("v_cur", b), ("Gs", b // 2)], writes=[("vst", h // 2)])
                        for pp in range(4):
                            if b % 2 == 1:
                                S.add("pool", lambda e, b=b, pp=pp, t0=t0: e.dma_start(
                                    out=v_d[pp, t0 + b * P: t0 + (b + 1) * P, :].rearrange("p (h d) -> p h d", h=2),
                                    in_=vst[:, 2 * pp:2 * pp + 2, :]),
                                    reads=[("vst", pp)], writes=[("v_d", pp, ti)], dma=True)
                            else:
                                S.add("pool", lambda e, b=b, pp=pp, t0=t0: e.dma_start(
                                    out=v_d[pp, t0 + b * P: t0 + (b + 1) * P, :].rearrange("p (h d) -> p h d", h=2),
                                    in_=v_cur[:, b, 2 * pp:2 * pp + 2, :]),
                                    reads=[("v_cur", b)], writes=[("v_d", pp, ti)], dma=True)

                ckpt(5)
                if ti + 1 < n_tiles:
                    load_x(ti + 1)

                for b in range(NB):
                    for h in range(4):
                        S.add("pool", lambda e, b=b, h=h: e.tensor_scalar(out=vw[:, b, h, :], in0=mv_aug[:, b, h, :],
                                                                            scalar1=es_[:, b, h:h + 1], scalar2=None, op0=ALU.mult),
                              reads=[("mv_aug", b), ("es", b)], writes=[("vw", b)])
                ml_steps = []
                MB = [6, 7]

                def stageA(b):
                    par = b % 2

                    def sT():
                        ba = MB[0]

                        def ftr(e):
                            for h in range(4):
                                i = e.transpose(out=psb_b[ba][:, h * P:(h + 1) * P], in_=fT2[:, 4 + h, b * P:(b + 1) * P],
                                                identity=identb[:])
                            return i
                        S.add("pe", ftr, reads=[("fT2", 4 + h) for h in range(4)] + ["identb"], writes=[("bank", ba)])
                        S.add("dve", lambda e: e.tensor_copy(out=ktok[:, par, :, :],
                                                             in_=psb_b[ba][:, 0:512].rearrange("p (h d) -> p h d", h=4)),
                              reads=[("bank", ba)], writes=[("ktok", par)])

                    def sS():
                        bs_ = MB[1]

                        def fst(e):
                            for h in range(4):
                                i = e.matmul(psb[bs_][:, h * P:(h + 1) * P], lhsT=fT2[:, 4 + h, b * P:(b + 1) * P],
                                             rhs=fT2[:, h, b * P:(b + 1) * P], start=True, stop=True)
                            return i
                        S.add("pe", fst, reads=[("fT2", c) for c in range(8)], writes=[("bank", bs_)])
                        S.add("dve", lambda e: e.tensor_tensor(out=Sm[:, par, :, :],
                                                               in0=psb[bs_][:, :].rearrange("p (h d) -> p h d", h=4),
                                                               in1=tri4[:], op=ALU.mult),
                              reads=[("bank", bs_), "tri4"], writes=[("Sm", par)])
                    ml_steps.append(sT)
                    ml_steps.append(sS)

                def stageB(b):
                    par = b % 2

                    def sU(half):
                        ub = MB[half]

                        def fu(e):
                            for hq in range(2):
                                h = 2 * half + hq
                                i = e.matmul(psb[ub][:, hq * 129:(hq + 1) * 129], lhsT=ktok[:, par, h, :], rhs=vw[:, b, h, :],
                                             start=True, stop=True)
                            return i
                        S.add("pe", fu, reads=[("ktok", par), ("vw", b)], writes=[("bank", ub)])
                        for hq in range(2):
                            h = 2 * half + hq
                            S.add("dve", lambda e, h=h: e.tensor_scalar(out=Cst[:, h, :], in0=Cst[:, h, :],
                                                                         scalar1=eg_[:, b, h:h + 1], scalar2=None, op0=ALU.mult),
                                  reads=[("Cst", h), ("eg", b)], writes=[("Cst", h)])
                            S.add("dve", lambda e, h=h, hq=hq: e.scalar_tensor_tensor(
                                out=Cst[:, h, :], in0=psb[ub][:, hq * 129:(hq + 1) * 129], scalar=eg_[:, b, h:h + 1], in1=Cst[:, h, :],
                                op0=ALU.mult, op1=ALU.add),
                                reads=[("bank", ub), ("Cst", h), ("eg", b)], writes=[("Cst", h)])

                    def sO(half):
                        ob = MB[half]

                        def fo_(e):
                            for hq in range(2):
                                h = 2 * half + hq
                                e.matmul(psb[ob][:, hq * 129:(hq + 1) * 129], lhsT=Sm[:, par, h, :], rhs=vw[:, b, h, :],
                                         start=True, stop=False)
                                i = e.matmul(psb[ob][:, hq * 129:(hq + 1) * 129], lhsT=fT2[:, h, b * P:(b + 1) * P],
                                             rhs=Cb[:, h, :], start=False, stop=True)
                            return i
                        S.add("pe", fo_, reads=[("Sm", par), ("vw", b), ("Cb", 2 * half), ("Cb", 2 * half + 1)]
                              + [("fT2", 2 * half), ("fT2", 2 * half + 1)], writes=[("bank", ob)])
                        for hq in range(2):
                            h = 2 * half + hq
                            S.add("dve", lambda e, h=h, hq=hq: e.tensor_scalar(out=nd[:, h, :], in0=psb[ob][:, hq * 129:(hq + 1) * 129],
                                                                                scalar1=eb_[:, b, h:h + 1], scalar2=None, op0=ALU.mult),
                                  reads=[("bank", ob), ("eb", b)], writes=[("nd", h)])

                    def sTail():
                        for h in range(4):
                            S.add("pool", lambda e, h=h: e.tensor_copy(out=Cb[:, h, :], in_=Cst[:, h, :]),
                                  reads=[("Cst", h)], writes=[("Cb", h)])
                        NDT = [("nd", h) for h in range(4)]
                        S.add("dve", lambda e: e.tensor_scalar(out=sm1[:, 0, :], in0=nd[:, :, 128], scalar1=-1.0, scalar2=None, op0=ALU.mult),
                              reads=NDT, writes=["sm1"])
                        S.add("dve", lambda e: e.scalar_tensor_tensor(out=sm1[:, 0, :], in0=nd[:, :, 128], scalar=1.0, in1=sm1[:, 0, :],
                                                                      op0=ALU.max, op1=ALU.max),
                              reads=NDT + ["sm1"], writes=["sm1"])
                        S.add("dve", lambda e: e.reciprocal(out=sm1[:, 1, :], in_=sm1[:, 0, :]), reads=["sm1"], writes=["sm1"])
                        for h in range(4):
                            S.add("act", lambda e, h=h: e.activation(out=t1[:, h, :], in_=nd[:, h, 0:128], func=AF.Square,
                                                                      scale=sm1[:, 1, h:h + 1], accum_out=sm1[:, 2, h:h + 1]),
                                  reads=[("nd", h), "sm1"], writes=[("t1", h), ("sm1b", h)])
                        SB4 = [("sm1b", h) for h in range(4)]
                        S.add("act", lambda e: e.activation(out=sm1[:, 3, :], in_=sm1[:, 2, :], func=AF.Ln, scale=1.0 / 128, bias=EPS),
                              reads=SB4, writes=["sm1c"])
                        S.add("act", lambda e: e.activation(out=sm1[:, 3, :], in_=sm1[:, 3, :], func=AF.Exp, scale=-0.5),
                              reads=["sm1c"], writes=["sm1c"])
                        S.add("dve", lambda e: e.tensor_tensor(out=sm1[:, 3, :], in0=sm1[:, 3, :], in1=sm1[:, 1, :], op=ALU.mult),
                              reads=["sm1c", "sm1"], writes=["sm1c"])
                        for h in range(4):
                            S.add("dve", lambda e, h=h: e.scalar_tensor_tensor(out=tok[:, b, h * 128:(h + 1) * 128], in0=nd[:, h, 0:128],
                                                                                scalar=sm1[:, 3, h:h + 1],
                                                                                in1=og[:, b, h * 128:(h + 1) * 128],
                                                                                op0=ALU.mult, op1=ALU.mult),
                                  reads=[("nd", h), "sm1c", ("og", b), ("t1", h)], writes=[("tok", b)])
                    ml_steps.append(lambda: sO(0))
                    ml_steps.append(lambda: sO(1))
                    ml_steps.append(lambda: sU(0))
                    ml_steps.append(lambda: sU(1))
                    ml_steps.append(sTail)

                for b in range(NB):
                    stageA(b)
                    stageB(b)
                for c in range(4):
                    def sMix(c=c):
                        mb_ = MB[c % 2]

                        def trm(e):
                            for b in range(NB):
                                i = e.transpose(out=psb_b[mb_][:, b * P:(b + 1) * P],
                                                in_=tok[:, b, c * P:(c + 1) * P], identity=identb[:])
                            return i
                        S.add("pe", trm, reads=[("tok", b) for b in range(NB)] + ["identb"], writes=[("bank", mb_)])
                        S.add("dve", lambda e: e.tensor_copy(out=hT[:, c, :], in_=psb_b[mb_][:, 0:T]),
                              reads=[("bank", mb_)], writes=[("hT", c)])
                    ml_steps.append(sMix)

                ckpt(7)
                npast = ti * NB
                LOOK = 1
                ml_every = max(1, (32 + 16 * ti) // 36)
                pend = []
                deferred = []

                def tick():
                    for d in deferred:
                        d[0] -= 1
                    while deferred and deferred[0][0] <= 0:
                        deferred.pop(0)[1]()

                def push_unit(s_fn, s_reads, exp_fn, exp_reads, pv_fn):
                    nonlocal_state["u"] += 1
                    u = nonlocal_state["u"]
                    bank = 2 * (u % 2)
                    pt = u % 3
                    S.add("pe", lambda e, bank=bank: s_fn(e, bank), reads=s_reads, writes=[("bank", bank), ("bank", bank + 1)])
                    S.add("act", lambda e, bank=bank, pt=pt: exp_fn(e, bank, pt),
                          reads=[("bank", bank), ("bank", bank + 1)] + exp_reads, writes=[("PT2", pt)])
                    pend.append(lambda pt=pt: pv_fn(pt))
                    if len(pend) > LOOK:
                        pend.pop(0)()
                    tick()
                    if ml_steps and nonlocal_state["u"] % ml_every == 0:
                        ml_steps.pop(0)()

                for pp in range(4):
                    obank = [4, 5]
                    first = [True, True]
                    for c0 in range(0, npast, KCH):
                        nblk = min(KCH, npast - c0)
                        ki = next_wbuf()
                        kview = wbuf[ki][:, 0:nblk * P]
                        kdeps = [("kT_d", pp, tj) for tj in range(c0 // NB, (c0 + nblk) // NB)]
                        S.add("sp", lambda e, kview=kview, pp=pp, c0=c0, nblk=nblk: e.dma_start(
                            out=kview, in_=kT_d[pp, :, c0 * P:(c0 + nblk) * P]),
                            reads=kdeps, writes=[("wbuf", ki)], dma=True)
                        vi = next_wbuf()
                        vview = wbuf[vi][:, 0:nblk * 256].rearrange("p (j d) -> p j d", j=nblk)
                        vdeps = [("v_d", pp, tj) for tj in range(c0 // NB, (c0 + nblk) // NB)]
                        S.add("sp", lambda e, vview=vview, pp=pp, c0=c0, nblk=nblk: e.dma_start(
                            out=vview, in_=v_d[pp, c0 * P:(c0 + nblk) * P, :].rearrange("(j p) d -> p j d", p=P)),
                            reads=vdeps, writes=[("wbuf", vi)], dma=True)
                        for j in range(0, nblk, 2):
                            for hh in range(2):
                                h = 2 * pp + hh

                                def s_fn(e, bank, hh=hh, j=j, kview=kview, pp=pp):
                                    e.matmul(psb[bank][:, :], lhsT=kview[:, j * P:(j + 1) * P], rhs=fqz[:, pp, hh, :],
                                             start=True, stop=True)
                                    return e.matmul(psb[bank + 1][:, :], lhsT=kview[:, (j + 1) * P:(j + 2) * P], rhs=fqz[:, pp, hh, :],
                                                    start=True, stop=True)

                                def exp_fn(e, bank, pt, h=h, jb=c0 + j):
                                    return e.activation(out=PT2[pt][:, :, :], in_=psum[:, bank:bank + 2, :], func=AF.Exp, scale=0.125,
                                                        bias=bias_i[:, h, jb:jb + 1])

                                def pv_fn(pt, hh=hh, j=j, vview=vview, vi=vi, st=first[hh], ob=obank[hh]):
                                    def f(e):
                                        e.matmul(psb[ob][:, :], lhsT=vview[:, j, hh * P:(hh + 1) * P], rhs=PT2[pt][:, 0, :],
                                                 start=st, stop=False)
                                        return e.matmul(psb[ob][:, :], lhsT=vview[:, j + 1, hh * P:(hh + 1) * P], rhs=PT2[pt][:, 1, :],
                                                        start=False, stop=False)
                                    S.add("pe", f, reads=[("wbuf", vi), ("PT2", pt)], writes=[("bank", ob)])
                                push_unit(s_fn, [("wbuf", ki), ("fqz", pp)], exp_fn, [("bias_i", h)], pv_fn)
                                first[hh] = False
                    for jj in range(NB):
                        for hh in range(2):
                            h = 2 * pp + hh
                            pr = slice(hh * 64, hh * 64 + 64)
                            q0 = jj * P

                            def s_fn(e, bank, hh=hh, jj=jj, q0=q0, pp=pp):
                                e.matmul(psb[bank][:, q0:q0 + P], lhsT=identb[:], rhs=negmb[:], start=True, stop=False)
                                i = e.matmul(psb[bank][:, q0:q0 + P], lhsT=kT_cur[:, pp, jj * P:(jj + 1) * P],
                                             rhs=fqz[:, pp, hh, q0:q0 + P], start=False, stop=True)
                                if q0 + P < T:
                                    i = e.matmul(psb[bank][:, q0 + P:T], lhsT=kT_cur[:, pp, jj * P:(jj + 1) * P],
                                                 rhs=fqz[:, pp, hh, q0 + P:T], start=True, stop=True)
                                return i

                            def exp_fn(e, bank, pt, h=h, jb=npast + jj, q0=q0):
                                return e.activation(out=PT2[pt][:, 0, q0:T], in_=psb[bank][:, q0:T], func=AF.Exp, scale=0.125,
                                                    bias=bias_i[:, h, jb:jb + 1])

                            def pv_fn(pt, hh=hh, h=h, jj=jj, q0=q0, st=first[hh], ob=obank[hh], pp=pp):
                                S.add("pe", lambda e: e.matmul(psb[ob][:, q0:T], lhsT=v_cur[:, jj, h, :], rhs=PT2[pt][:, 0, q0:T],
                                                               start=st, stop=(jj == NB - 1)),
                                      reads=[("v_cur", jj), ("PT2", pt)], writes=[("bank", ob)])
                                if jj == NB - 1:
                                    dr = 64 if hh == 0 else 0
                                    rows = slice(0, 64) if hh == 0 else slice(64, 128)
                                    S.add("dve", lambda e: e.tensor_copy(out=osb[hh][:, :], in_=psb[ob][:, :]),
                                          reads=[("bank", ob)], writes=[("osb", hh)])
                                    S.add("dve", lambda e: e.reciprocal(out=osb[hh][dr:dr + 1, :], in_=osb[hh][dr:dr + 1, :]),
                                          reads=[("osb", hh)], writes=[("osb", hh)])

                                    def norm2():
                                        S.add("pe", lambda e: e.matmul(psb[6][:, :], lhsT=cs[dr:dr + 1, 256:384], rhs=osb[hh][dr:dr + 1, :],
                                                                       start=True, stop=True),
                                              reads=[("osb", hh), "cs"], writes=[("bank", 6)])
                                        S.add("dve", lambda e: e.tensor_tensor(out=hT[rows, 4 + pp, :], in0=psb[6][rows, :],
                                                                                in1=osb[hh][rows, :], op=ALU.mult),
                                              reads=[("bank", 6), ("osb", hh)], writes=[("hT", 4 + pp)])
                                    deferred.append([3, norm2])
                            push_unit(s_fn, [("kT_cur", pp), ("fqz", pp), "identb", "negmb"], exp_fn, [("bias_i", h)], pv_fn)
                            first[hh] = False
                while pend:
                    pend.pop(0)()
                    tick()
                while deferred:
                    deferred.pop(0)[1]()
                while ml_steps:
                    ml_steps.pop(0)()

                ckpt(8)
                pieces = []
                for n in range(2):
                    slot, wv = load_w(wpiece(wb_out, n * 512), (P, 8, 512), wtoks('wb_out', D))
                    pieces.append((slot, wv))
                for b in range(NB):
                    for n in range(2):
                        bank = gen_bank()
                        slot, wv = pieces[n]

                        def fo(e, b=b, wv=wv, bank=bank):
                            for k in range(8):
                                i = e.matmul(psb[bank][:, :], lhsT=hT[:, k, b * P:(b + 1) * P], rhs=wv[:, k, :],
                                             start=(k == 0), stop=(k == 7))
                            return i
                        S.add("pe", fo, reads=[("wbuf", slot)] + HT, writes=[("bank", bank)])
                        S.add("dve", lambda e, X=X, b=b, n=n, bank=bank: e.tensor_tensor(
                            out=X[:, b, n * 512:(n + 1) * 512], in0=psb[bank][:, :], in1=X[:, b, n * 512:(n + 1) * 512], op=ALU.add),
                            reads=[("bank", bank), XT[b]], writes=[XT[b]])

                ckpt(9)
                rms_to_hT(X, NB, 1, XT, T, hT, "hT")
                pt_rr = 0
                for piece in range(2):
                    slot, wv = load_w(wpiece(wb_xq, piece * 512), (P, 8, 512), wtoks('wb_xq', D))
                    for cc in range(4):
                        c = piece * 4 + cc
                        bank = gen_bank()
                        mm_feat(wv, slot, cc, hT, HT, T, bank)
                        S.add("act" if cc % 2 == 0 else "dve",
                              (lambda e, c=c, bank=bank: e.activation(out=fT2[:, c, :], in_=psb[bank][:, :], func=AF.Copy))
                              if cc % 2 == 0 else
                              (lambda e, c=c, bank=bank: e.tensor_copy(out=fT2[:, c, :], in_=psb[bank][:, :])),
                              reads=[("bank", bank)], writes=[("fT2", c)])
                for grp in range(2):
                    heads = [2 * grp, 2 * grp + 1]
                    sbanks = {}
                    for hq, h in enumerate(heads):
                        for mb in range(2):
                            bank = gen_bank()
                            sbanks[(hq, mb)] = bank

                            def fxs(e, h=h, mb=mb, bank=bank):
                                for cc in range(2):
                                    i = e.matmul(psb[bank][:, :], lhsT=memKT[:, 2 * h + cc, mb * P:(mb + 1) * P],
                                                 rhs=fT2[:, 2 * h + cc, :], start=(cc == 0), stop=(cc == 1))
                                return i
                            S.add("pe", fxs, reads=MEMK + [("fT2", 2 * h), ("fT2", 2 * h + 1)], writes=[("bank", bank)])
                    for hq, h in enumerate(heads):
                        for mb in range(2):
                            bank = sbanks[(hq, mb)]
                            pt = 2 * hq + mb
                            S.add("act", lambda e, bank=bank, pt=pt: e.activation(out=PT[pt], in_=psb[bank][:, :], func=AF.Exp,
                                                                                   scale=1.0 / 16),
                                  reads=[("bank", bank)], writes=[("PT2", pt // 2)])
                    for hq, h in enumerate(heads):
                        db = 6 + hq

                        def fden(e, hq=hq, db=db):
                            for mb in range(2):
                                i = e.matmul(psb[db][:, :], lhsT=onesb[:], rhs=PT[2 * hq + mb], start=(mb == 0), stop=(mb == 1))
                            return i
                        S.add("pe", fden, reads=["onesb", ("PT2", hq)], writes=[("bank", db)])
                    pvb = {}
                    for hq, h in enumerate(heads):
                        for cc in range(2):
                            ob = (4 + cc) if hq == 0 else gen_bank()
                            pvb[(hq, cc)] = ob

                            def fpv(e, h=h, hq=hq, cc=cc, ob=ob):
                                for mb in range(2):
                                    i = e.matmul(psb[ob][:, :], lhsT=memV[:, mb, (2 * h + cc) * P:(2 * h + cc + 1) * P],
                                                 rhs=PT[2 * hq + mb], start=(mb == 0), stop=(mb == 1))
                                return i
                            S.add("pe", fpv, reads=MEMV + [("PT2", hq)], writes=[("bank", ob)])
                    for hq, h in enumerate(heads):
                        db = 6 + hq
                        S.add("act", lambda e, hq=hq, db=db: e.activation(out=bc[hq][:], in_=psb[db][:, :], func=AF.Ln),
                              reads=[("bank", db)], writes=[("bc", hq)])
                        S.add("act", lambda e, hq=hq: e.activation(out=bc[hq][:], in_=bc[hq][:], func=AF.Exp, scale=-1.0),
                              reads=[("bc", hq)], writes=[("bc", hq)])
                    for hq, h in enumerate(heads):
                        for cc in range(2):
                            ob = pvb[(hq, cc)]
                            S.add("dve", lambda e, h=h, hq=hq, cc=cc, ob=ob: e.tensor_tensor(out=hT[:, 2 * h + cc, :], in0=psb[ob][:, :],
                                                                                              in1=bc[hq][:], op=ALU.mult),
                                  reads=[("bank", ob), ("bc", hq)], writes=[("hT", 2 * h + cc)])
                for n in range(2):
                    slot, wv = load_w(wpiece(wb_xo, n * 512), (P, 8, 512), wtoks('wb_xo', D))
                    for b in range(NB):
                        bank = gen_bank()
                        mm_tok(wv, slot, hT, HT, b, bank)
                        S.add("dve", lambda e, X=X, b=b, n=n, bank=bank: e.tensor_tensor(
                            out=X[:, b, n * 512:(n + 1) * 512], in0=psb[bank][:, :], in1=X[:, b, n * 512:(n + 1) * 512], op=ALU.add),
                            reads=[("bank", bank), XT[b]], writes=[XT[b]])

                ckpt(10)
                rms_to_hT(X, NB, 2, XT, T, hT, "hT")
                ybank_rr = 0
                ffn_w2 = {}

                def ffn1(g):
                    s1, w1 = load_w(wpiece(wb_ff1, g * 512), (P, 8, 512), wtoks('wb_ff1', D))
                    i2 = next_wbuf()
                    w2 = wbuf[i2][:, 0:4096].rearrange("p (a b) -> p a b", a=4)
                    S.add("sp", lambda e: e.dma_start(
                        out=w2, in_=wb_ff2[g * 512:(g + 1) * 512, :].rearrange("(c p) n -> p c n", p=P)),
                        reads=wtoks("wb_ff2", DFF), writes=[("wbuf", i2)], dma=True)
                    ffn_w2[g] = (i2, w2)
                    u = uT[g % 2]
                    for cc in range(4):
                        bank = gen_bank()
                        mm_feat(w1, s1, cc, hT, HT, T, bank)
                        r = rl[cc % 2]
                        S.add("act", lambda e, r=r, bank=bank: e.activation(out=r[:], in_=psb[bank][:, :], func=AF.Relu),
                              reads=[("bank", bank)], writes=[("rl", cc % 2)])
                        S.add("pool", lambda e, r=r, u=u, cc=cc: e.tensor_tensor(out=u[:, cc, :], in0=r[:], in1=r[:], op=ALU.mult),
                              reads=[("rl", cc % 2)], writes=[("uT", g % 2, cc)])

                def ffn2(g, X=X):
                    nonlocal_state["yb"] = nonlocal_state.get("yb", 0)
                    i2, w2 = ffn_w2[g]
                    u = uT[g % 2]
                    for b in range(NB):
                        for n in range(2):
                            yb = 4 + nonlocal_state["yb"]
                            nonlocal_state["yb"] = (nonlocal_state["yb"] + 1) % 2

                            def fy(e, b=b, n=n, yb=yb):
                                for cc in range(4):
                                    i = e.matmul(psb[yb][:, :], lhsT=u[:, cc, b * P:(b + 1) * P], rhs=w2[:, cc, n * 512:(n + 1) * 512],
                                                 start=(cc == 0), stop=(cc == 3))
                                return i
                            S.add("pe", fy, reads=[("wbuf", i2)] + [("uT", g % 2, cc) for cc in range(4)], writes=[("bank", yb)])
                            S.add("dve", lambda e, b=b, n=n, yb=yb: e.tensor_tensor(
                                out=X[:, b, n * 512:(n + 1) * 512], in0=psb[yb][:, :], in1=X[:, b, n * 512:(n + 1) * 512], op=ALU.add),
                                reads=[("bank", yb), XT[b]], writes=[XT[b]])

                ffn1(0)
                for g in range(8):
                    if g + 1 < 8:
                        ffn1(g + 1)
                    ffn2(g)

                ckpt(11)
                def final_norm(X=X, XT=XT, t0=t0):
                    for b in range(NB):
                        S.add("act", lambda e, b=b: e.activation(out=junk[:], in_=X[:, b, :], func=AF.Square, accum_out=ssq2[:, b:b + 1]),
                              reads=[XT[b]], writes=["junk", ("ssq2", b)])
                        S.add("act", lambda e, b=b: e.activation(out=rstd2[:, b:b + 1], in_=ssq2[:, b:b + 1], func=AF.Ln,
                                                                  scale=1.0 / D, bias=EPS),
                              reads=[("ssq2", b)], writes=[("rstd2", b)])
                        S.add("act", lambda e, b=b: e.activation(out=rstd2[:, b:b + 1], in_=rstd2[:, b:b + 1], func=AF.Exp, scale=-0.5),
                              reads=[("rstd2", b)], writes=[("rstd2", b)])
                        S.add("dve", lambda e, b=b: e.scalar_tensor_tensor(out=X[:, b, :], in0=X[:, b, :], scalar=rstd2[:, b:b + 1],
                                                                            in1=lnf_rep[:], op0=ALU.mult, op1=ALU.mult),
                              reads=[XT[b], ("rstd2", b), "lnf_rep"], writes=[XT[b]])
                    S.add("pool", lambda e: e.dma_start(out=out_d[t0:t0 + T, :].rearrange("(b p) d -> p b d", p=P), in_=X[:]),
                          reads=XT, dma=True)
                pending_final.append(final_norm)

            while pending_final:
                pending_final.pop(0)()

        except StopBuild:
            pass
        sems = {}
        for name in sorted(S.cnt.keys()):
            sems[name] = es.enter_context(nc.semaphore(name))
        with nc.Block() as block:
            S.emit_all(block, sems)
    return nc


def make_consts():
    ident = np.eye(P, dtype=np.float32)
    tri = np.triu(np.ones((P, P), np.float32))
    ones = np.ones((P, P), np.float32)
    negm = (tri - 1.0) * 30000.0
    return np.ascontiguousarray(np.concatenate([ident, tri, ones, negm], axis=1))


_NC_CACHE = {}


def kernel(**inputs):
    n_cores = 8
    f = lambda a: np.ascontiguousarray(np.asarray(a, dtype=np.float32))
    x = f(inputs["x"])
    B, S_, _ = x.shape
    n_tiles = S_ // T
    if n_tiles not in _NC_CACHE:
        _NC_CACHE[n_tiles] = build_nc(n_tiles)
    nc = _NC_CACHE[n_tiles]
    shared = {
        "ln1": f(inputs["ln1"])[0], "w_in": f(inputs["w_in"])[0], "ml_conv_w": f(inputs["ml_conv_w"])[0],
        "ml_conv_b": f(inputs["ml_conv_b"])[0], "ml_b_i": f(inputs["ml_b_i"])[0], "ml_b_f": f(inputs["ml_b_f"])[0],
        "ml_norm": f(inputs["ml_norm"])[0], "fx_b_f": f(inputs["fx_b_f"])[0], "w_out": f(inputs["w_out"])[0],
        "ln_x": f(inputs["ln_x"])[0], "ln_mem": f(inputs["ln_mem"])[0], "w_xq": f(inputs["w_xq"])[0],
        "w_xkv": f(inputs["w_xkv"])[0], "w_xo": f(inputs["w_xo"])[0], "ln2": f(inputs["ln2"])[0],
        "w_ff1": f(inputs["w_ff1"])[0], "w_ff2": f(inputs["w_ff2"])[0], "ln_f": f(inputs["ln_f"]),
        "cst": make_consts(),
    }
    mem = f(inputs["mem"])
    in_maps = []
    for c in range(n_cores):
        m = dict(shared)
        m["x"] = x[c]
        m["mem"] = mem[c]
        in_maps.append(m)
    res = run_bass_kernel_spmd(nc, in_maps, core_ids=list(range(n_cores)))
    return np.stack([res.results[c]["out"] for c in range(n_cores)], axis=0).astype(np.float32)
```

```python
from contextlib import ExitStack
import numpy as np
import concourse.bass as bass
import concourse.mybir as mybir
from concourse.bass_utils import run_bass_kernel_spmd

F32 = mybir.dt.float32
BF16 = mybir.dt.bfloat16
AF = mybir.ActivationFunctionType
ALU = mybir.AluOpType

ENGS = ["pe", "act", "dve", "pool", "sp"]
N_DMA_SEMS = 8

P = 128
D = 1024
SEQ = 8192
T = 512
NB = T // P
NMEM = 256
DFF = 4096
INC = 3600
EPS = 1e-6
NW = 5
WB_ELEMS = 4160
KCH = 16
LN_ALPHA = float(-0.5 * np.log(128.0))


class Sched:
    def __init__(self):
        self.streams = {e: [] for e in ENGS}
        self.known = {e: {} for e in ENGS}
        self.cnt = {}
        self.opinfo = []
        self.last_w = {}
        self.readers = {}
        self.dma_rr = {e: 0 for e in ENGS}

    def add(self, eng, emit, reads=(), writes=(), dma=False):
        idx = len(self.opinfo)
        bank_toks = [t for t in reads if isinstance(t, tuple) and t[0] == "bank"]
        if bank_toks:
            reads = [t for t in reads if not (isinstance(t, tuple) and t[0] == "bank")]
            writes = list(writes) + bank_toks
        deps = set()
        for t in reads:
            w = self.last_w.get(t)
            if w is not None:
                deps.add(w)
        for t in writes:
            w = self.last_w.get(t)
            if w is not None:
                deps.add(w)
            for r in self.readers.get(t, ()):
                deps.add(r)
        for t in reads:
            self.readers.setdefault(t, []).append(idx)
        for t in writes:
            self.last_w[t] = idx
            self.readers[t] = []
        known = self.known[eng]
        waits = []
        for d in sorted(deps, reverse=True):
            sem, val, clock = self.opinfo[d]
            if known.get(sem, 0) >= val:
                continue
            waits.append((sem, val))
            for k, v in clock.items():
                if known.get(k, 0) < v:
                    known[k] = v
        if dma:
            sem = "dma_%s_%d" % (eng, self.dma_rr[eng])
            self.dma_rr[eng] = (self.dma_rr[eng] + 1) % N_DMA_SEMS
            prev = self.cnt.get(sem, 0)
            if known.get(sem, 0) < prev:
                waits.append((sem, prev))
                known[sem] = prev
            val = prev + 16
            inc = 16
        else:
            sem = eng
            val = self.cnt.get(sem, 0) + 1
            inc = 1
        self.cnt[sem] = val
        clock = dict(known)
        clock[sem] = val
        self.opinfo.append((sem, val, clock))
        self.streams[eng].append((waits, emit, sem, inc))
        return idx

    def emit_all(self, block, sems):
        def run(engname):
            def body(eng):
                for waits, emit, sem, inc in self.streams[engname]:
                    for (s, v) in waits:
                        eng.wait_ge(sems[s], v)
                    inst = emit(eng)
                    inst.then_inc(sems[sem], inc)
                if engname == "sp":
                    for s, v in self.cnt.items():
                        eng.wait_ge(sems[s], v)
            return body
        block.tensor(run("pe"))
        block.scalar(run("act"))
        block.vector(run("dve"))
        block.gpsimd(run("pool"))
        block.sync(run("sp"))


def build_nc(n_tiles=SEQ // T, dbg=False):
    nc = bass.Bass("TRN2", target_bir_lowering=False)
    S_LEN = n_tiles * T

    def din(name, shape):
        return nc.dram_tensor(name, shape, F32, kind="ExternalInput").ap()

    x_d = din("x", [S_LEN, D])
    mem_d = din("mem", [NMEM, D])
    ln1_d = din("ln1", [D])
    w_in_d = din("w_in", [D, INC])
    convw_d = din("ml_conv_w", [4, D])
    convb_d = din("ml_conv_b", [D])
    bi_d = din("ml_b_i", [4])
    bf_d = din("ml_b_f", [4])
    mlnorm_d = din("ml_norm", [512])
    fxbf_d = din("fx_b_f", [8])
    w_out_d = din("w_out", [D, D])
    lnx_d = din("ln_x", [D])
    lnmem_d = din("ln_mem", [D])
    w_xq_d = din("w_xq", [D, D])
    w_xkv_d = din("w_xkv", [D, 2 * D])
    w_xo_d = din("w_xo", [D, D])
    ln2_d = din("ln2", [D])
    w_ff1_d = din("w_ff1", [D, DFF])
    w_ff2_d = din("w_ff2", [DFF, D])
    lnf_d = din("ln_f", [D])
    cst_d = din("cst", [P, 512])
    out_d = nc.dram_tensor("out", [S_LEN, D], F32, kind="ExternalOutput").ap()

    def dscr(name, shape, dt=BF16):
        return nc.dram_tensor(name, shape, dt, kind="Internal").ap()

    wb_in = dscr("wb_in", [D, INC])
    wb_out = dscr("wb_out", [D, D])
    wb_xq = dscr("wb_xq", [D, D])
    wb_xkv = dscr("wb_xkv", [D, 2 * D])
    wb_xo = dscr("wb_xo", [D, D])
    wb_ff1 = dscr("wb_ff1", [D, DFF])
    wb_ff2 = dscr("wb_ff2", [DFF, D])
    kT_d = dscr("kT_d", [4, P, S_LEN])
    v_d = dscr("v_d", [4, S_LEN, 256])

    S = Sched()
    es = ExitStack()
    import os
    STAGE = int(os.environ.get('K_STAGE', '99'))

    class StopBuild(Exception):
        pass

    def ckpt(n):
        if STAGE <= n:
            raise StopBuild()
    with es:
        def sb(name, shape, dt):
            return es.enter_context(nc.sbuf_tensor(name, shape, dt))

        xt = [sb("xt%d" % i, [P, NB, D], F32) for i in range(2)]
        hT = sb("hT", [P, 8, T], BF16)
        tok = sb("tok", [P, NB, D], BF16)
        junk = sb("junk", [P, D], BF16)
        fT2 = sb("fT2", [P, 8, T], BF16)
        stage = [sb("stage%d" % i, [P, T + 3], F32) for i in range(2)]
        acc = [sb("acc%d" % i, [P, T], F32) for i in range(2)]
        halo = sb("halo", [P, 8, 3], F32)
        mv_aug = sb("mv_aug", [P, NB, 4, 129], BF16)
        og = sb("og", [P, NB, 512], F32)
        fqz = sb("fqz", [P, 4, 2, T], BF16)
        kT_cur = sb("kT_cur", [P, 4, T], BF16)
        v_cur = sb("v_cur", [P, NB, 8, P], BF16)
        PT2 = [sb("PT%d" % i, [P, 2, T], BF16) for i in range(3)]
        PT = [PT2[0][:, 0, :], PT2[0][:, 1, :], PT2[1][:, 0, :], PT2[1][:, 1, :]]
        vst = sb("vst", [P, 8, P], BF16)
        Gs = sb("Gs", [P, 2, 8], F32)
        nonlocal_state = {"u": 0}
        osb = [sb("osb%d" % i, [P, T], F32) for i in range(2)]
        bc = [sb("bc%d" % i, [P, T], F32) for i in range(2)]
        uT = [sb("uT%d" % i, [P, 4, T], BF16) for i in range(2)]
        rl = [sb("rl%d" % i, [P, T], BF16) for i in range(2)]
        wbuf = [sb("wbuf%d" % i, [P, WB_ELEMS], BF16) for i in range(NW)]
        memKT = sb("memKT", [P, 8, NMEM], BF16)
        memV = sb("memV", [P, 2, D], BF16)
        cs = sb("cs", [P, 512], F32)
        identb = sb("identb", [P, P], BF16)
        negmb = sb("negmb", [P, P], BF16)
        onesb = sb("onesb", [P, P], BF16)
        lnf_rep = sb("lnf_rep", [P, D], F32)
        mln_rep = sb("mln_rep", [P, 512], F32)
        gcol = sb("gcol", [P, 4, 8], F32)
        cw = sb("cw", [P, 4, 8], F32)
        cb = sb("cb", [P, 8], F32)
        gbias = sb("gbias", [P, 16], F32)
        wg = sb("wg", [P, 8, 16], BF16)
        negc = sb("negc", [P, 8, SEQ // P], F32)
        bias_i = sb("bias_i", [P, 8, SEQ // P], F32)
        spaccs = sb("spaccs", [P, NB + 1, 8], F32)
        crefn = sb("crefn", [P, 8], F32)
        Cst = sb("Cst", [P, 4, 129], F32)
        Cb = sb("Cb", [P, 4, 129], BF16)
        ssq = sb("ssq", [P, NB], F32)
        ssq2 = sb("ssq2", [P, NB], F32)
        rstd2 = sb("rstd2", [P, NB], F32)
        rstd = sb("rstd", [P, NB], F32)
        gs = sb("gs", [P, NB, 16], F32)
        sp_ = sb("sp_", [P, NB, 12], F32)
        gsm = sb("gsm", [P, NB, 24], F32)
        es_ = sb("es_", [P, NB, 4], F32)
        eb_ = sb("eb_", [P, NB, 4], F32)
        eg_ = sb("eg_", [P, NB, 4], F32)
        ktok = sb("ktok", [P, 2, 4, P], BF16)
        Sm = sb("Sm", [P, 2, 4, P], BF16)
        vw = sb("vw", [P, NB, 4, 129], BF16)
        tri4 = sb("tri4", [P, 4, P], F32)
        nd = sb("nd", [P, 4, 129], F32)
        sm1 = sb("sm1", [P, 4, 4], F32)
        t1 = sb("t1", [P, 4, P], BF16)
        memx = xt[1]

        psum = es.enter_context(nc.psum_tensor("psum", [P, 8, 512], F32))
        psb = [psum[:, i, :] for i in range(8)]
        ptrs = [psum[:, 6, :].bitcast(BF16), psum[:, 7, :].bitcast(BF16)]
        psb_b = [psum[:, i, :].bitcast(BF16) for i in range(8)]
        pm = psum[:, 7, :]

        gen_rr = [0]

        def gen_bank():
            b = gen_rr[0]
            gen_rr[0] = (gen_rr[0] + 1) % 4
            return b

        wb_rr = [0]

        def next_wbuf():
            i = wb_rr[0]
            wb_rr[0] = (wb_rr[0] + 1) % NW
            return i

        def load_w(src_ap, shape3, deps=()):
            i = next_wbuf()
            n = shape3[1] * shape3[2]
            view = wbuf[i][:, 0:n].rearrange("p (a b) -> p a b", a=shape3[1])
            S.add("sp", lambda e, v=view, s=src_ap: e.dma_start(out=v, in_=s), reads=list(deps),
                  writes=[("wbuf", i)], dma=True)
            return i, view

        def wpiece(wd, c0, ncols=512):
            return wd.rearrange("(c p) n -> p c n", p=P)[:, :, c0:c0 + ncols]

        try:
            def cast_w(src, dst, rows, name):
                for r0 in range(0, rows, 256):
                    S.add("pool", lambda e, r0=r0: e.dma_start(out=dst[r0:r0 + 256, :], in_=src[r0:r0 + 256, :]),
                          writes=[(name, r0)], dma=True)
            WTOK = {}

            def wtoks(name, rows):
                return [(name, r0) for r0 in range(0, rows, 256)]
            S.add("pool", lambda e: e.memset(halo[:], 0.0), writes=["halo"])
            S.add("pool", lambda e: e.memset(spaccs[:], 0.0), writes=[("spaccs", b) for b in range(NB + 1)])
            S.add("pool", lambda e: e.memset(Cst[:], 0.0), writes=[("Cst", h) for h in range(4)])
            S.add("pool", lambda e: e.memset(Cb[:], 0.0), writes=[("Cb", h) for h in range(4)])
            S.add("pool", lambda e: e.memset(mv_aug[:], 1.0), writes=[("mv_aug", b) for b in range(NB)])
            S.add("pool", lambda e: e.memset(v_cur[:], 0.0), writes=[("v_cur", b) for b in range(NB)])
            v_cur5 = v_cur[:].rearrange("p b (pp hh) c -> p b pp hh c", hh=2)
            S.add("pool", lambda e: e.memset(v_cur5[:, :, :, 0, 64:65], 1.0), writes=[("v_cur", b) for b in range(NB)])
            S.add("pool", lambda e: e.memset(v_cur5[:, :, :, 1, 0:1], 1.0), writes=[("v_cur", b) for b in range(NB)])
            S.add("pool", lambda e: e.memset(fqz[:], 0.0), writes=[("fqz", c) for c in range(4)])

            cast_w(w_xkv_d, wb_xkv, D, "wb_xkv")
            cast_w(w_in_d, wb_in, D, "wb_in")
            cast_w(w_out_d, wb_out, D, "wb_out")
            cast_w(w_xq_d, wb_xq, D, "wb_xq")
            cast_w(w_xo_d, wb_xo, D, "wb_xo")
            cast_w(w_ff1_d, wb_ff1, D, "wb_ff1")
            cast_w(w_ff2_d, wb_ff2, DFF, "wb_ff2")

            def ld(dst, src, tokname, slow=False):
                if slow:
                    S.add("sp", lambda e: e.dma_start(out=dst, in_=src, allow_slow_non_contiguous=True),
                          writes=[tokname], dma=True)
                else:
                    S.add("sp", lambda e: e.dma_start(out=dst, in_=src), writes=[tokname], dma=True)

            ld(cs[:], cst_d, "cs")
            for i, g in enumerate([ln1_d, lnx_d, ln2_d, lnmem_d]):
                ld(gcol[:, i, :], g.rearrange("(c p) -> p c", p=P), ("gcol", i), slow=True)
            for j in range(4):
                ld(cw[:, j, :], convw_d[j, :].rearrange("(c p) -> p c", p=P), ("cw", j), slow=True)
            ld(cb[:], convb_d.rearrange("(c p) -> p c", p=P), "cb", slow=True)
            ld(gbias[:, 0:4], bi_d.partition_broadcast(P), ("gbias", 0))
            ld(gbias[:, 4:8], bf_d.partition_broadcast(P), ("gbias", 1))
            ld(gbias[:, 8:16], fxbf_d.partition_broadcast(P), ("gbias", 2))
            ld(lnf_rep[:], lnf_d.partition_broadcast(P), "lnf_rep")
            ld(mln_rep[:], mlnorm_d.partition_broadcast(P), "mln_rep")
            S.add("sp", lambda e: e.dma_start(out=memx[:, 0:2, :], in_=mem_d.rearrange("(b p) d -> p b d", p=P)), writes=[("xt", 1, 0), ("xt", 1, 1)], dma=True)
            S.add("sp", lambda e: e.dma_start(out=wg[:, :, 0:8], in_=wpiece(wb_in, 2048, 8),
                                             allow_slow_non_contiguous=True),
                  reads=wtoks("wb_in", D), writes=[("wg", 0)], dma=True)
            S.add("sp", lambda e: e.dma_start(out=wg[:, :, 8:16], in_=wpiece(wb_in, 3592, 8),
                                             allow_slow_non_contiguous=True),
                  reads=wtoks("wb_in", D), writes=[("wg", 1)], dma=True)
            W_IN = wtoks("wb_in", D)
            GB = [("gbias", 0), ("gbias", 1), ("gbias", 2)]
            WG = [("wg", 0), ("wg", 1)]

            S.add("dve", lambda e: e.tensor_copy(out=identb[:], in_=cs[:, 0:128]), reads=["cs"], writes=["identb"])
            S.add("dve", lambda e: e.tensor_copy(out=onesb[:], in_=cs[:, 256:384]), reads=["cs"], writes=["onesb"])
            S.add("dve", lambda e: e.tensor_copy(out=negmb[:], in_=cs[:, 384:512]), reads=["cs"], writes=["negmb"])
            tri_f = cs[:, 128:256]
            for h in range(4):
                S.add("dve", lambda e, h=h: e.tensor_copy(out=tri4[:, h, :], in_=cs[:, 128:256]), reads=["cs"], writes=["tri4"])
            ones_f = cs[:, 256:384]
            def rms_to_hT(src, nblk, gi, src_toks, ntok, dstT, dst_tok):
                rms_front(src, nblk, src_toks)
                rms_back(nblk, gi, ntok, dstT, dst_tok)

            def rms_front(src, nblk, src_toks):
                for b in range(nblk):
                    S.add("act", lambda e, b=b: e.activation(out=junk[:], in_=src[:, b, :], func=AF.Square,
                                                              accum_out=ssq[:, b:b + 1]),
                          reads=[src_toks[b]], writes=["junk", ("ssq", b)])
                    S.add("act", lambda e, b=b: e.activation(out=rstd[:, b:b + 1], in_=ssq[:, b:b + 1], func=AF.Ln,
                                                              scale=1.0 / D, bias=EPS),
                          reads=[("ssq", b)], writes=[("rstd", b)])
                    S.add("act", lambda e, b=b: e.activation(out=rstd[:, b:b + 1], in_=rstd[:, b:b + 1], func=AF.Exp, scale=-0.5),
                          reads=[("rstd", b)], writes=[("rstd", b)])
                    S.add("dve", lambda e, b=b: e.tensor_scalar(
                        out=tok[:, b, :], in0=src[:, b, :], scalar1=rstd[:, b:b + 1], scalar2=None, op0=ALU.mult),
                        reads=[src_toks[b], ("rstd", b)], writes=[("tok", b)])
            def rms_back(nblk, gi, ntok, dstT, dst_tok):
                for c in range(8):
                    half = c % 2

                    def tr(e, c=c, half=half):
                        for b in range(nblk):
                            i = e.transpose(out=ptrs[half][:, b * P:(b + 1) * P],
                                            in_=tok[:, b, c * P:(c + 1) * P], identity=identb[:])
                        return i
                    S.add("pe", tr, reads=[("tok", b) for b in range(nblk)] + ["identb"], writes=[("bank", 6 + half)])
                    eng = "act" if c % 2 == 0 else "dve"
                    if eng == "act":
                        S.add("act", lambda e, c=c, half=half: e.activation(
                            out=dstT[:, c, 0:ntok], in_=ptrs[half][:, 0:ntok], func=AF.Copy,
                            scale=gcol[:, gi, c:c + 1]),
                            reads=[("bank", 6 + half), ("gcol", gi)], writes=[(dst_tok, c)])
                    else:
                        S.add("dve", lambda e, c=c, half=half: e.tensor_scalar(
                            out=dstT[:, c, 0:ntok], in0=ptrs[half][:, 0:ntok],
                            scalar1=gcol[:, gi, c:c + 1], scalar2=None, op0=ALU.mult),
                            reads=[("bank", 6 + half), ("gcol", gi)], writes=[(dst_tok, c)])

            def mm_feat(wview, wslot, cc, actT, act_toks, ntok, bank):
                def f(e):
                    for k in range(8):
                        i = e.matmul(psb[bank][:, 0:ntok], lhsT=wview[:, k, cc * P:(cc + 1) * P], rhs=actT[:, k, 0:ntok],
                                     start=(k == 0), stop=(k == 7))
                    return i
                S.add("pe", f, reads=[("wbuf", wslot)] + act_toks, writes=[("bank", bank)])

            def mm_tok(wview, wslot, actT, act_toks, b, bank, ncols=512, c0=0, extra_reads=()):
                def f(e):
                    for k in range(8):
                        i = e.matmul(psb[bank][:, 0:ncols], lhsT=actT[:, k, b * P:(b + 1) * P],
                                     rhs=wview[:, k, c0:c0 + ncols], start=(k == 0), stop=(k == 7))
                    return i
                S.add("pe", f, reads=[("wbuf", wslot)] + act_toks + list(extra_reads), writes=[("bank", bank)])

            HT = [("hT", c) for c in range(8)]

            ckpt(0)
            rms_to_hT(memx, 2, 3, [("xt", 1, 0), ("xt", 1, 1)], NMEM, hT, "hT")
            def load_w_dep(src_ap, shape3, dep_toks):
                i = next_wbuf()
                n = shape3[1] * shape3[2]
                view = wbuf[i][:, 0:n].rearrange("p (a b) -> p a b", a=shape3[1])
                S.add("sp", lambda e, v=view, s=src_ap: e.dma_start(out=v, in_=s), reads=dep_toks,
                      writes=[("wbuf", i)], dma=True)
                return i, view

            for piece in range(2):
                slot, wv = load_w_dep(wpiece(wb_xkv, piece * 512), (P, 8, 512), wtoks("wb_xkv", D))
                for cc in range(4):
                    bank = gen_bank()
                    mm_feat(wv, slot, cc, hT, HT, NMEM, bank)
                    c = piece * 4 + cc
                    S.add("act", lambda e, c=c, bank=bank: e.activation(out=memKT[:, c, :], in_=psb[bank][:, 0:NMEM],
                                                                         func=AF.Copy),
                          reads=[("bank", bank)], writes=[("memKT", c)])
            for piece in range(2):
                slot, wv = load_w_dep(wpiece(wb_xkv, D + piece * 512), (P, 8, 512), wtoks("wb_xkv", D))
                for mb in range(2):
                    bank = gen_bank()
                    mm_tok(wv, slot, hT, HT, mb, bank)
                    S.add("dve", lambda e, mb=mb, piece=piece, bank=bank: e.tensor_copy(
                        out=memV[:, mb, piece * 512:(piece + 1) * 512], in_=psb[bank][:, :]),
                        reads=[("bank", bank)], writes=[("memV", mb, piece)])
            MEMK = [("memKT", c) for c in range(8)]
            MEMV = [("memV", mb, pc) for mb in range(2) for pc in range(2)]

            ckpt(1)
            def load_x(i):
                xb = xt[i % 2]
                S.add("sp", lambda e: e.dma_start(out=xb[:], in_=x_d[i * T:(i + 1) * T, :].rearrange("(b p) d -> p b d", p=P)),
                      writes=[("xt", i % 2, b) for b in range(NB)], dma=True)

            pending_final = []
            load_x(0)
            for ti in range(n_tiles):
                X = xt[ti % 2]
                XT = [("xt", ti % 2, b) for b in range(NB)]
                t0 = ti * T

                if ti == 0:
                    rms_front(X, NB, XT)
                rms_back(NB, 0, T, hT, "hT")
                while pending_final:
                    pending_final.pop(0)()

                ckpt(2)
                for b in range(NB):
                    gb = 4 + b

                    def fg(e, b=b, gb=gb):
                        for k in range(8):
                            i = e.matmul(psb[gb][:, 0:16], lhsT=hT[:, k, b * P:(b + 1) * P],
                                         rhs=wg[:, k, :], start=(k == 0), stop=(k == 7))
                        return i
                    S.add("pe", fg, reads=HT + WG, writes=[("bank", gb)])
                    S.add("dve", lambda e, b=b, gb=gb: e.tensor_tensor(out=gs[:, b, :], in0=psb[gb][:, 0:16],
                                                                        in1=gbias[:], op=ALU.add),
                          reads=[("bank", gb)] + GB, writes=[("gs", b)])
                    S.add("act", lambda e, b=b: e.activation(out=sp_[:, b, :], in_=gs[:, b, 4:16], func=AF.Exp, scale=-1.0),
                          reads=[("gs", b)], writes=[("sp", b)])
                    S.add("act", lambda e, b=b: e.activation(out=sp_[:, b, :], in_=sp_[:, b, :], func=AF.Ln, bias=1.0),
                          reads=[("sp", b)], writes=[("sp", b)])

                for b in range(NB):
                    S.add("dve", lambda e, b=b: e.tensor_tensor(out=spaccs[:, b + 1, :], in0=spaccs[:, b, :], in1=sp_[:, b, 4:12],
                                                                 op=ALU.add),
                          reads=[("spaccs", b), ("sp", b)], writes=[("spaccs", b + 1)])
                for piece in range(2):
                    slot, wv = load_w(wpiece(wb_in, piece * 512), (P, 8, 512), W_IN)
                    for cc in range(4):
                        c = piece * 4 + cc
                        bank = gen_bank()
                        mm_feat(wv, slot, cc, hT, HT, T, bank)
                        st = stage[c % 2]
                        ac = acc[c % 2]
                        stn = ("stage", c % 2)
                        acn = ("acc", c % 2)
                        S.add("pool", lambda e, c=c, st=st: e.tensor_copy(out=st[:, 0:3], in_=halo[:, c, :]),
                              reads=["halo"], writes=[stn])
                        S.add("act", lambda e, st=st, bank=bank: e.activation(out=st[:, 3:T + 3], in_=psb[bank][:, :],
                                                                               func=AF.Copy),
                              reads=[("bank", bank)], writes=[stn])
                        S.add("pool", lambda e, c=c, st=st: e.tensor_copy(out=halo[:, c, :], in_=st[:, T:T + 3]),
                              reads=[stn], writes=["halo"])
                        S.add("act", lambda e, c=c, ac=ac, bank=bank: e.activation(
                            out=ac[:], in_=psb[bank][:, :], func=AF.Identity, scale=cw[:, 3, c:c + 1], bias=cb[:, c:c + 1]),
                            reads=[("bank", bank), ("cw", 3), "cb"], writes=[acn])
                        for j in (2, 1, 0):
                            S.add("dve", lambda e, c=c, st=st, ac=ac, j=j: e.scalar_tensor_tensor(
                                out=ac[:], in0=st[:, j:T + j], scalar=cw[:, j, c:c + 1], in1=ac[:],
                                op0=ALU.mult, op1=ALU.add),
                                reads=[stn, ("cw", j), acn], writes=[acn])
                        S.add("act", lambda e, c=c, ac=ac: e.activation(out=fT2[:, c, :], in_=ac[:], func=AF.Silu),
                              reads=[acn], writes=[("fT2", c)])
                ckpt(4)
                for b in range(NB):
                    gb = 4 + b

                    def fcs(e, b=b, gb=gb):
                        e.matmul(psb[gb][:, 16:28], lhsT=tri_f, rhs=sp_[:, b, :], start=True, stop=True)
                        e.matmul(psb[gb][:, 28:32], lhsT=ones_f, rhs=sp_[:, b, 0:4], start=True, stop=True)
                        e.matmul(psb[gb][:, 32:40], lhsT=tri_f, rhs=sp_[:, b, 4:12], start=True, stop=False)
                        i = e.matmul(psb[gb][:, 32:40], lhsT=ones_f, rhs=spaccs[:, b, :], start=False, stop=True)
                        if b == 1:
                            i = e.matmul(psb[gb][:, 40:48], lhsT=ones_f, rhs=spaccs[:, 2, :], start=True, stop=True)
                        return i
                    S.add("pe", fcs, reads=["cs", ("sp", b), ("spaccs", b)] + ([("spaccs", 2)] if b == 1 else []),
                          writes=[("bank", gb)])
                    S.add("dve", lambda e, b=b, gb=gb: e.tensor_copy(out=gsm[:, b, :], in_=psb[gb][:, 16:40]),
                          reads=[("bank", gb)], writes=[("gsm", b)])
                    if b == 1:
                        S.add("dve", lambda e, gb=gb: e.tensor_copy(out=crefn[:], in_=psb[gb][:, 40:48]),
                              reads=[("bank", gb)], writes=["crefn"])
                    blk = ti * NB + b
                    S.add("dve", lambda e, blk=blk, b=b: e.tensor_copy(out=negc[:, :, blk], in_=gsm[:, b, 16:24]),
                          reads=[("gsm", b)], writes=["negc"])
                    if b % 2 == 1:
                        S.add("dve", lambda e, b=b: e.tensor_tensor(out=Gs[:, b // 2, :], in0=gsm[:, b, 16:24], in1=gsm[:, b - 1, 16:24],
                                                                     op=ALU.subtract),
                              reads=[("gsm", b), ("gsm", b - 1)], writes=[("Gs", b // 2)])
                        S.add("act", lambda e, b=b: e.activation(out=Gs[:, b // 2, :], in_=Gs[:, b // 2, :], func=AF.Exp),
                              reads=[("Gs", b // 2)], writes=[("Gs", b // 2)])
                    S.add("dve", lambda e, b=b: e.tensor_tensor(out=es_[:, b, :], in0=gsm[:, b, 0:4], in1=gs[:, b, 0:4],
                                                                 op=ALU.add),
                          reads=[("gsm", b), ("gs", b)], writes=[("es", b)])
                    S.add("act", lambda e, b=b: e.activation(out=es_[:, b, :], in_=es_[:, b, :], func=AF.Exp,
                                                              bias=LN_ALPHA),
                          reads=[("es", b)], writes=[("es", b)])
                    S.add("act", lambda e, b=b: e.activation(out=eb_[:, b, :], in_=gsm[:, b, 0:4], func=AF.Exp, scale=-1.0),
                          reads=[("gsm", b)], writes=[("eb", b)])
                    S.add("act", lambda e, b=b: e.activation(out=eg_[:, b, :], in_=gsm[:, b, 12:16], func=AF.Exp, scale=-1.0),
                          reads=[("gsm", b)], writes=[("eg", b)])
                nblk_tot = (ti + 1) * NB
                for h in range(8):
                    S.add("dve", lambda e, h=h, nblk_tot=nblk_tot: e.tensor_scalar(out=bias_i[:, h, 0:nblk_tot], in0=negc[:, h, 0:nblk_tot],
                                                                 scalar1=crefn[:, h:h + 1], scalar2=None, op0=ALU.subtract),
                          reads=["negc", "crefn"], writes=[("bias_i", h)])
                S.add("dve", lambda e: e.tensor_copy(out=spaccs[:, 0, :], in_=spaccs[:, NB, :]),
                      reads=[("spaccs", NB)], writes=[("spaccs", 0)])
                slot, wv = load_w(wpiece(wb_in, 1024), (P, 8, 512), W_IN)
                for b in range(NB):
                    bank = gen_bank()
                    mm_tok(wv, slot, hT, HT, b, bank)
                    S.add("dve", lambda e, b=b, bank=bank: e.tensor_copy(
                        out=mv_aug[:, b, :, 0:128], in_=psb[bank][:, :].rearrange("p (h d) -> p h d", h=4)),
                        reads=[("bank", bank)], writes=[("mv_aug", b)])
                slot, wv = load_w(wpiece(wb_in, 1536), (P, 8, 512), W_IN)
                for b in range(NB):
                    bank = gen_bank()
                    mm_tok(wv, slot, hT, HT, b, bank)
                    S.add("act", lambda e, b=b, bank=bank: e.activation(out=og[:, b, :], in_=psb[bank][:, :], func=AF.Sigmoid),
                          reads=[("bank", bank)], writes=[("og", b)])
                    S.add("pool", lambda e, b=b: e.tensor_tensor(out=og[:, b, :], in0=og[:, b, :], in1=mln_rep[:], op=ALU.mult),
                          reads=[("og", b), "mln_rep"], writes=[("og", b)])
                slot, wv = load_w(wpiece(wb_in, 2056), (P, 8, 512), W_IN)
                for cc in range(4):
                    bank = gen_bank()
                    mm_feat(wv, slot, cc, hT, HT, T, bank)
                    S.add("dve", lambda e, cc=cc, bank=bank: e.tensor_copy(out=fqz[0:64, cc, 0, :], in_=psb[bank][0:64, :]),
                          reads=[("bank", bank)], writes=[("fqz", cc)])
                    S.add("dve", lambda e, cc=cc, bank=bank: e.tensor_copy(out=fqz[64:128, cc, 1, :], in_=psb[bank][64:128, :]),
                          reads=[("bank", bank)], writes=[("fqz", cc)])
                slot, wv = load_w(wpiece(wb_in, 2568), (P, 8, 512), W_IN)
                for cc in range(4):
                    bank = gen_bank()
                    mm_feat(wv, slot, cc, hT, HT, T, bank)
                    S.add("act", lambda e, cc=cc, bank=bank: e.activation(out=kT_cur[:, cc, :], in_=psb[bank][:, :], func=AF.Copy),
                          reads=[("bank", bank)], writes=[("kT_cur", cc)])
                    if ti < n_tiles - 1:
                        S.add("pool", lambda e, cc=cc, t0=t0: e.dma_start(out=kT_d[cc, :, t0:t0 + T], in_=kT_cur[:, cc, :]),
                              reads=[("kT_cur", cc)], writes=[("kT_d", cc, ti)], dma=True)
                slot, wv = load_w(wpiece(wb_in, 3080), (P, 8, 512), W_IN)
                for b in range(NB):
                    bank = gen_bank()
                    mm_tok(wv, slot, hT, HT, b, bank)
                    for hh in range(2):
                        S.add("dve", lambda e, b=b, bank=bank, hh=hh: e.tensor_copy(
                            out=v_cur5[:, b, :, hh, hh * 64:hh * 64 + 64],
                            in_=psb[bank][:, :].rearrange("p (pp hh d) -> p pp hh d", pp=4, hh=2)[:, :, hh, :]),
                            reads=[("bank", bank)], writes=[("v_cur", b)])
                    if ti < n_tiles - 1:
                        if b % 2 == 1:
                            for h in range(8):
                                S.add("dve", lambda e, b=b, h=h: e.tensor_scalar(out=vst[:, h, :], in0=v_cur[:, b, h, :],
                                                                                  scalar1=Gs[:, b // 2, h:h + 1], scalar2=None, op0=ALU.mult),
                                      reads=[("v_cur", b), ("Gs", b // 2)], writes=[("vst", h // 2)])
                        for pp in range(4):
                            if b % 2 == 1:
                                S.add("pool", lambda e, b=b, pp=pp, t0=t0: e.dma_start(
                                    out=v_d[pp, t0 + b * P: t0 + (b + 1) * P, :].rearrange("p (h d) -> p h d", h=2),
                                    in_=vst[:, 2 * pp:2 * pp + 2, :]),
                                    reads=[("vst", pp)], writes=[("v_d", pp, ti)], dma=True)
                            else:
                                S.add("pool", lambda e, b=b, pp=pp, t0=t0: e.dma_start(
                                    out=v_d[pp, t0 + b * P: t0 + (b + 1) * P, :].rearrange("p (h d) -> p h d", h=2),
                                    in_=v_cur[:, b, 2 * pp:2 * pp + 2, :]),
                                    reads=[("v_cur", b)], writes=[("v_d", pp, ti)], dma=True)

                ckpt(5)
                if ti + 1 < n_tiles:
                    load_x(ti + 1)

                for b in range(NB):
                    for h in range(4):
                        S.add("pool", lambda e, b=b, h=h: e.tensor_scalar(out=vw[:, b, h, :], in0=mv_aug[:, b, h, :],
                                                                            scalar1=es_[:, b, h:h + 1], scalar2=None, op0=ALU.mult),
                              reads=[("mv_aug", b), ("es", b)], writes=[("vw", b)])
                ml_steps = []
                MB = [6, 7]

                def stageA(b):
                    par = b % 2

                    def sT():
                        ba = MB[0]

                        def ftr(e):
                            for h in range(4):
                                i = e.transpose(out=psb_b[ba][:, h * P:(h + 1) * P], in_=fT2[:, 4 + h, b * P:(b + 1) * P],
                                                identity=identb[:])
                            return i
                        S.add("pe", ftr, reads=[("fT2", 4 + h) for h in range(4)] + ["identb"], writes=[("bank", ba)])
                        S.add("dve", lambda e: e.tensor_copy(out=ktok[:, par, :, :],
                                                             in_=psb_b[ba][:, 0:512].rearrange("p (h d) -> p h d", h=4)),
                              reads=[("bank", ba)], writes=[("ktok", par)])

                    def sS():
                        bs_ = MB[1]

                        def fst(e):
                            for h in range(4):
                                i = e.matmul(psb[bs_][:, h * P:(h + 1) * P], lhsT=fT2[:, 4 + h, b * P:(b + 1) * P],
                                             rhs=fT2[:, h, b * P:(b + 1) * P], start=True, stop=True)
                            return i
                        S.add("pe", fst, reads=[("fT2", c) for c in range(8)], writes=[("bank", bs_)])
                        S.add("dve", lambda e: e.tensor_tensor(out=Sm[:, par, :, :],
                                                               in0=psb[bs_][:, :].rearrange("p (h d) -> p h d", h=4),
                                                               in1=tri4[:], op=ALU.mult),
                              reads=[("bank", bs_), "tri4"], writes=[("Sm", par)])
                    ml_steps.append(sT)
                    ml_steps.append(sS)

                def stageB(b):
                    par = b % 2

                    def sU(half):
                        ub = MB[half]

                        def fu(e):
                            for hq in range(2):
                                h = 2 * half + hq
                                i = e.matmul(psb[ub][:, hq * 129:(hq + 1) * 129], lhsT=ktok[:, par, h, :], rhs=vw[:, b, h, :],
                                             start=True, stop=True)
                            return i
                        S.add("pe", fu, reads=[("ktok", par), ("vw", b)], writes=[("bank", ub)])
                        for hq in range(2):
                            h = 2 * half + hq
                            S.add("dve", lambda e, h=h: e.tensor_scalar(out=Cst[:, h, :], in0=Cst[:, h, :],
                                                                         scalar1=eg_[:, b, h:h + 1], scalar2=None, op0=ALU.mult),
                                  reads=[("Cst", h), ("eg", b)], writes=[("Cst", h)])
                            S.add("dve", lambda e, h=h, hq=hq: e.scalar_tensor_tensor(
                                out=Cst[:, h, :], in0=psb[ub][:, hq * 129:(hq + 1) * 129], scalar=eg_[:, b, h:h + 1], in1=Cst[:, h, :],
                                op0=ALU.mult, op1=ALU.add),
                                reads=[("bank", ub), ("Cst", h), ("eg", b)], writes=[("Cst", h)])

                    def sO(half):
                        ob = MB[half]

                        def fo_(e):
                            for hq in range(2):
                                h = 2 * half + hq
                                e.matmul(psb[ob][:, hq * 129:(hq + 1) * 129], lhsT=Sm[:, par, h, :], rhs=vw[:, b, h, :],
                                         start=True, stop=False)
                                i = e.matmul(psb[ob][:, hq * 129:(hq + 1) * 129], lhsT=fT2[:, h, b * P:(b + 1) * P],
                                             rhs=Cb[:, h, :], start=False, stop=True)
                            return i
                        S.add("pe", fo_, reads=[("Sm", par), ("vw", b), ("Cb", 2 * half), ("Cb", 2 * half + 1)]
                              + [("fT2", 2 * half), ("fT2", 2 * half + 1)], writes=[("bank", ob)])
                        for hq in range(2):
                            h = 2 * half + hq
                            S.add("dve", lambda e, h=h, hq=hq: e.tensor_scalar(out=nd[:, h, :], in0=psb[ob][:, hq * 129:(hq + 1) * 129],
                                                                                scalar1=eb_[:, b, h:h + 1], scalar2=None, op0=ALU.mult),
                                  reads=[("bank", ob), ("eb", b)], writes=[("nd", h)])

                    def sTail():
                        for h in range(4):
                            S.add("pool", lambda e, h=h: e.tensor_copy(out=Cb[:, h, :], in_=Cst[:, h, :]),
                                  reads=[("Cst", h)], writes=[("Cb", h)])
                        NDT = [("nd", h) for h in range(4)]
                        S.add("dve", lambda e: e.tensor_scalar(out=sm1[:, 0, :], in0=nd[:, :, 128], scalar1=-1.0, scalar2=None, op0=ALU.mult),
                              reads=NDT, writes=["sm1"])
                        S.add("dve", lambda e: e.scalar_tensor_tensor(out=sm1[:, 0, :], in0=nd[:, :, 128], scalar=1.0, in1=sm1[:, 0, :],
                                                                      op0=ALU.max, op1=ALU.max),
                              reads=NDT + ["sm1"], writes=["sm1"])
                        S.add("dve", lambda e: e.reciprocal(out=sm1[:, 1, :], in_=sm1[:, 0, :]), reads=["sm1"], writes=["sm1"])
                        for h in range(4):
                            S.add("act", lambda e, h=h: e.activation(out=t1[:, h, :], in_=nd[:, h, 0:128], func=AF.Square,
                                                                      scale=sm1[:, 1, h:h + 1], accum_out=sm1[:, 2, h:h + 1]),
                                  reads=[("nd", h), "sm1"], writes=[("t1", h), ("sm1b", h)])
                        SB4 = [("sm1b", h) for h in range(4)]
                        S.add("act", lambda e: e.activation(out=sm1[:, 3, :], in_=sm1[:, 2, :], func=AF.Ln, scale=1.0 / 128, bias=EPS),
                              reads=SB4, writes=["sm1c"])
                        S.add("act", lambda e: e.activation(out=sm1[:, 3, :], in_=sm1[:, 3, :], func=AF.Exp, scale=-0.5),
                              reads=["sm1c"], writes=["sm1c"])
                        S.add("dve", lambda e: e.tensor_tensor(out=sm1[:, 3, :], in0=sm1[:, 3, :], in1=sm1[:, 1, :], op=ALU.mult),
                              reads=["sm1c", "sm1"], writes=["sm1c"])
                        for h in range(4):
                            S.add("dve", lambda e, h=h: e.scalar_tensor_tensor(out=tok[:, b, h * 128:(h + 1) * 128], in0=nd[:, h, 0:128],
                                                                                scalar=sm1[:, 3, h:h + 1],
                                                                                in1=og[:, b, h * 128:(h + 1) * 128],
                                                                                op0=ALU.mult, op1=ALU.mult),
                                  reads=[("nd", h), "sm1c", ("og", b), ("t1", h)], writes=[("tok", b)])
                    ml_steps.append(lambda: sO(0))
                    ml_steps.append(lambda: sO(1))
                    ml_steps.append(lambda: sU(0))
                    ml_steps.append(lambda: sU(1))
                    ml_steps.append(sTail)

                for b in range(NB):
                    stageA(b)
                    stageB(b)
                for c in range(4):
                    def sMix(c=c):
                        mb_ = MB[c % 2]

                        def trm(e):
                            for b in range(NB):
                                i = e.transpose(out=psb_b[mb_][:, b * P:(b + 1) * P],
                                                in_=tok[:, b, c * P:(c + 1) * P], identity=identb[:])
                            return i
                        S.add("pe", trm, reads=[("tok", b) for b in range(NB)] + ["identb"], writes=[("bank", mb_)])
                        S.add("dve", lambda e: e.tensor_copy(out=hT[:, c, :], in_=psb_b[mb_][:, 0:T]),
                              reads=[("bank", mb_)], writes=[("hT", c)])
                    ml_steps.append(sMix)

                ckpt(7)
                npast = ti * NB
                LOOK = 1
                ml_every = max(1, (32 + 16 * ti) // 36)
                pend = []
                deferred = []

                def tick():
                    for d in deferred:
                        d[0] -= 1
                    while deferred and deferred[0][0] <= 0:
                        deferred.pop(0)[1]()

                def push_unit(s_fn, s_reads, exp_fn, exp_reads, pv_fn):
                    nonlocal_state["u"] += 1
                    u = nonlocal_state["u"]
                    bank = 2 * (u % 2)
                    pt = u % 3
                    S.add("pe", lambda e, bank=bank: s_fn(e, bank), reads=s_reads, writes=[("bank", bank), ("bank", bank + 1)])
                    S.add("act", lambda e, bank=bank, pt=pt: exp_fn(e, bank, pt),
                          reads=[("bank", bank), ("bank", bank + 1)] + exp_reads, writes=[("PT2", pt)])
                    pend.append(lambda pt=pt: pv_fn(pt))
                    if len(pend) > LOOK:
                        pend.pop(0)()
                    tick()
                    if ml_steps and nonlocal_state["u"] % ml_every == 0:
                        ml_steps.pop(0)()

                for pp in range(4):
                    obank = [4, 5]
                    first = [True, True]
                    for c0 in range(0, npast, KCH):
                        nblk = min(KCH, npast - c0)
                        ki = next_wbuf()
                        kview = wbuf[ki][:, 0:nblk * P]
                        kdeps = [("kT_d", pp, tj) for tj in range(c0 // NB, (c0 + nblk) // NB)]
                        S.add("sp", lambda e, kview=kview, pp=pp, c0=c0, nblk=nblk: e.dma_start(
                            out=kview, in_=kT_d[pp, :, c0 * P:(c0 + nblk) * P]),
                            reads=kdeps, writes=[("wbuf", ki)], dma=True)
                        vi = next_wbuf()
                        vview = wbuf[vi][:, 0:nblk * 256].rearrange("p (j d) -> p j d", j=nblk)
                        vdeps = [("v_d", pp, tj) for tj in range(c0 // NB, (c0 + nblk) // NB)]
                        S.add("sp", lambda e, vview=vview, pp=pp, c0=c0, nblk=nblk: e.dma_start(
                            out=vview, in_=v_d[pp, c0 * P:(c0 + nblk) * P, :].rearrange("(j p) d -> p j d", p=P)),
                            reads=vdeps, writes=[("wbuf", vi)], dma=True)
                        for j in range(0, nblk, 2):
                            for hh in range(2):
                                h = 2 * pp + hh

                                def s_fn(e, bank, hh=hh, j=j, kview=kview, pp=pp):
                                    e.matmul(psb[bank][:, :], lhsT=kview[:, j * P:(j + 1) * P], rhs=fqz[:, pp, hh, :],
                                             start=True, stop=True)
                                    return e.matmul(psb[bank + 1][:, :], lhsT=kview[:, (j + 1) * P:(j + 2) * P], rhs=fqz[:, pp, hh, :],
                                                    start=True, stop=True)

                                def exp_fn(e, bank, pt, h=h, jb=c0 + j):
                                    return e.activation(out=PT2[pt][:, :, :], in_=psum[:, bank:bank + 2, :], func=AF.Exp, scale=0.125,
                                                        bias=bias_i[:, h, jb:jb + 1])

                                def pv_fn(pt, hh=hh, j=j, vview=vview, vi=vi, st=first[hh], ob=obank[hh]):
                                    def f(e):
                                        e.matmul(psb[ob][:, :], lhsT=vview[:, j, hh * P:(hh + 1) * P], rhs=PT2[pt][:, 0, :],
                                                 start=st, stop=False)
                                        return e.matmul(psb[ob][:, :], lhsT=vview[:, j + 1, hh * P:(hh + 1) * P], rhs=PT2[pt][:, 1, :],
                                                        start=False, stop=False)
                                    S.add("pe", f, reads=[("wbuf", vi), ("PT2", pt)], writes=[("bank", ob)])
                                push_unit(s_fn, [("wbuf", ki), ("fqz", pp)], exp_fn, [("bias_i", h)], pv_fn)
                                first[hh] = False
                    for jj in range(NB):
                        for hh in range(2):
                            h = 2 * pp + hh
                            pr = slice(hh * 64, hh * 64 + 64)
                            q0 = jj * P

                            def s_fn(e, bank, hh=hh, jj=jj, q0=q0, pp=pp):
                                e.matmul(psb[bank][:, q0:q0 + P], lhsT=identb[:], rhs=negmb[:], start=True, stop=False)
                                i = e.matmul(psb[bank][:, q0:q0 + P], lhsT=kT_cur[:, pp, jj * P:(jj + 1) * P],
                                             rhs=fqz[:, pp, hh, q0:q0 + P], start=False, stop=True)
                                if q0 + P < T:
                                    i = e.matmul(psb[bank][:, q0 + P:T], lhsT=kT_cur[:, pp, jj * P:(jj + 1) * P],
                                                 rhs=fqz[:, pp, hh, q0 + P:T], start=True, stop=True)
                                return i

                            def exp_fn(e, bank, pt, h=h, jb=npast + jj, q0=q0):
                                return e.activation(out=PT2[pt][:, 0, q0:T], in_=psb[bank][:, q0:T], func=AF.Exp, scale=0.125,
                                                    bias=bias_i[:, h, jb:jb + 1])

                            def pv_fn(pt, hh=hh, h=h, jj=jj, q0=q0, st=first[hh], ob=obank[hh], pp=pp):
                                S.add("pe", lambda e: e.matmul(psb[ob][:, q0:T], lhsT=v_cur[:, jj, h, :], rhs=PT2[pt][:, 0, q0:T],
                                                               start=st, stop=(jj == NB - 1)),
                                      reads=[("v_cur", jj), ("PT2", pt)], writes=[("bank", ob)])
                                if jj == NB - 1:
                                    dr = 64 if hh == 0 else 0
                                    rows = slice(0, 64) if hh == 0 else slice(64, 128)
                                    S.add("dve", lambda e: e.tensor_copy(out=osb[hh][:, :], in_=psb[ob][:, :]),
                                          reads=[("bank", ob)], writes=[("osb", hh)])
                                    S.add("dve", lambda e: e.reciprocal(out=osb[hh][dr:dr + 1, :], in_=osb[hh][dr:dr + 1, :]),
                                          reads=[("osb", hh)], writes=[("osb", hh)])

                                    def norm2():
                                        S.add("pe", lambda e: e.matmul(psb[6][:, :], lhsT=cs[dr:dr + 1, 256:384], rhs=osb[hh][dr:dr + 1, :],
                                                                       start=True, stop=True),
                                              reads=[("osb", hh), "cs"], writes=[("bank", 6)])
                                        S.add("dve", lambda e: e.tensor_tensor(out=hT[rows, 4 + pp, :], in0=psb[6][rows, :],
                                                                                in1=osb[hh][rows, :], op=ALU.mult),
                                              reads=[("bank", 6), ("osb", hh)], writes=[("hT", 4 + pp)])
                                    deferred.append([3, norm2])
                            push_unit(s_fn, [("kT_cur", pp), ("fqz", pp), "identb", "negmb"], exp_fn, [("bias_i", h)], pv_fn)
                            first[hh] = False
                while pend:
                    pend.pop(0)()
                    tick()
                while deferred:
                    deferred.pop(0)[1]()
                while ml_steps:
                    ml_steps.pop(0)()

                ckpt(8)
                pieces = []
                for n in range(2):
                    slot, wv = load_w(wpiece(wb_out, n * 512), (P, 8, 512), wtoks('wb_out', D))
                    pieces.append((slot, wv))
                for b in range(NB):
                    for n in range(2):
                        bank = gen_bank()
                        slot, wv = pieces[n]

                        def fo(e, b=b, wv=wv, bank=bank):
                            for k in range(8):
                                i = e.matmul(psb[bank][:, :], lhsT=hT[:, k, b * P:(b + 1) * P], rhs=wv[:, k, :],
                                             start=(k == 0), stop=(k == 7))
                            return i
                        S.add("pe", fo, reads=[("wbuf", slot)] + HT, writes=[("bank", bank)])
                        S.add("dve", lambda e, X=X, b=b, n=n, bank=bank: e.tensor_tensor(
                            out=X[:, b, n * 512:(n + 1) * 512], in0=psb[bank][:, :], in1=X[:, b, n * 512:(n + 1) * 512], op=ALU.add),
                            reads=[("bank", bank), XT[b]], writes=[XT[b]])

                ckpt(9)
                rms_to_hT(X, NB, 1, XT, T, hT, "hT")
                pt_rr = 0
                for piece in range(2):
                    slot, wv = load_w(wpiece(wb_xq, piece * 512), (P, 8, 512), wtoks('wb_xq', D))
                    for cc in range(4):
                        c = piece * 4 + cc
                        bank = gen_bank()
                        mm_feat(wv, slot, cc, hT, HT, T, bank)
                        S.add("act" if cc % 2 == 0 else "dve",
                              (lambda e, c=c, bank=bank: e.activation(out=fT2[:, c, :], in_=psb[bank][:, :], func=AF.Copy))
                              if cc % 2 == 0 else
                              (lambda e, c=c, bank=bank: e.tensor_copy(out=fT2[:, c, :], in_=psb[bank][:, :])),
                              reads=[("bank", bank)], writes=[("fT2", c)])
                for grp in range(2):
                    heads = [2 * grp, 2 * grp + 1]
                    sbanks = {}
                    for hq, h in enumerate(heads):
                        for mb in range(2):
                            bank = gen_bank()
                            sbanks[(hq, mb)] = bank

                            def fxs(e, h=h, mb=mb, bank=bank):
                                for cc in range(2):
                                    i = e.matmul(psb[bank][:, :], lhsT=memKT[:, 2 * h + cc, mb * P:(mb + 1) * P],
                                                 rhs=fT2[:, 2 * h + cc, :], start=(cc == 0), stop=(cc == 1))
                                return i
                            S.add("pe", fxs, reads=MEMK + [("fT2", 2 * h), ("fT2", 2 * h + 1)], writes=[("bank", bank)])
                    for hq, h in enumerate(heads):
                        for mb in range(2):
                            bank = sbanks[(hq, mb)]
                            pt = 2 * hq + mb
                            S.add("act", lambda e, bank=bank, pt=pt: e.activation(out=PT[pt], in_=psb[bank][:, :], func=AF.Exp,
                                                                                   scale=1.0 / 16),
                                  reads=[("bank", bank)], writes=[("PT2", pt // 2)])
                    for hq, h in enumerate(heads):
                        db = 6 + hq

                        def fden(e, hq=hq, db=db):
                            for mb in range(2):
                                i = e.matmul(psb[db][:, :], lhsT=onesb[:], rhs=PT[2 * hq + mb], start=(mb == 0), stop=(mb == 1))
                            return i
                        S.add("pe", fden, reads=["onesb", ("PT2", hq)], writes=[("bank", db)])
                    pvb = {}
                    for hq, h in enumerate(heads):
                        for cc in range(2):
                            ob = (4 + cc) if hq == 0 else gen_bank()
                            pvb[(hq, cc)] = ob

                            def fpv(e, h=h, hq=hq, cc=cc, ob=ob):
                                for mb in range(2):
                                    i = e.matmul(psb[ob][:, :], lhsT=memV[:, mb, (2 * h + cc) * P:(2 * h + cc + 1) * P],
                                                 rhs=PT[2 * hq + mb], start=(mb == 0), stop=(mb == 1))
                                return i
                            S.add("pe", fpv, reads=MEMV + [("PT2", hq)], writes=[("bank", ob)])
                    for hq, h in enumerate(heads):
                        db = 6 + hq
                        S.add("act", lambda e, hq=hq, db=db: e.activation(out=bc[hq][:], in_=psb[db][:, :], func=AF.Ln),
                              reads=[("bank", db)], writes=[("bc", hq)])
                        S.add("act", lambda e, hq=hq: e.activation(out=bc[hq][:], in_=bc[hq][:], func=AF.Exp, scale=-1.0),
                              reads=[("bc", hq)], writes=[("bc", hq)])
                    for hq, h in enumerate(heads):
                        for cc in range(2):
                            ob = pvb[(hq, cc)]
                            S.add("dve", lambda e, h=h, hq=hq, cc=cc, ob=ob: e.tensor_tensor(out=hT[:, 2 * h + cc, :], in0=psb[ob][:, :],
                                                                                              in1=bc[hq][:], op=ALU.mult),
                                  reads=[("bank", ob), ("bc", hq)], writes=[("hT", 2 * h + cc)])
                for n in range(2):
                    slot, wv = load_w(wpiece(wb_xo, n * 512), (P, 8, 512), wtoks('wb_xo', D))
                    for b in range(NB):
                        bank = gen_bank()
                        mm_tok(wv, slot, hT, HT, b, bank)
                        S.add("dve", lambda e, X=X, b=b, n=n, bank=bank: e.tensor_tensor(
                            out=X[:, b, n * 512:(n + 1) * 512], in0=psb[bank][:, :], in1=X[:, b, n * 512:(n + 1) * 512], op=ALU.add),
                            reads=[("bank", bank), XT[b]], writes=[XT[b]])

                ckpt(10)
                rms_to_hT(X, NB, 2, XT, T, hT, "hT")
                ybank_rr = 0
                ffn_w2 = {}

                def ffn1(g):
                    s1, w1 = load_w(wpiece(wb_ff1, g * 512), (P, 8, 512), wtoks('wb_ff1', D))
                    i2 = next_wbuf()
                    w2 = wbuf[i2][:, 0:4096].rearrange("p (a b) -> p a b", a=4)
                    S.add("sp", lambda e: e.dma_start(
                        out=w2, in_=wb_ff2[g * 512:(g + 1) * 512, :].rearrange("(c p) n -> p c n", p=P)),
                        reads=wtoks("wb_ff2", DFF), writes=[("wbuf", i2)], dma=True)
                    ffn_w2[g] = (i2, w2)
                    u = uT[g % 2]
                    for cc in range(4):
                        bank = gen_bank()
                        mm_feat(w1, s1, cc, hT, HT, T, bank)
                        r = rl[cc % 2]
                        S.add("act", lambda e, r=r, bank=bank: e.activation(out=r[:], in_=psb[bank][:, :], func=AF.Relu),
                              reads=[("bank", bank)], writes=[("rl", cc % 2)])
                        S.add("pool", lambda e, r=r, u=u, cc=cc: e.tensor_tensor(out=u[:, cc, :], in0=r[:], in1=r[:], op=ALU.mult),
                              reads=[("rl", cc % 2)], writes=[("uT", g % 2, cc)])

                def ffn2(g, X=X):
                    nonlocal_state["yb"] = nonlocal_state.get("yb", 0)
                    i2, w2 = ffn_w2[g]
                    u = uT[g % 2]
                    for b in range(NB):
                        for n in range(2):
                            yb = 4 + nonlocal_state["yb"]
                            nonlocal_state["yb"] = (nonlocal_state["yb"] + 1) % 2

                            def fy(e, b=b, n=n, yb=yb):
                                for cc in range(4):
                                    i = e.matmul(psb[yb][:, :], lhsT=u[:, cc, b * P:(b + 1) * P], rhs=w2[:, cc, n * 512:(n + 1) * 512],
                                                 start=(cc == 0), stop=(cc == 3))
                                return i
                            S.add("pe", fy, reads=[("wbuf", i2)] + [("uT", g % 2, cc) for cc in range(4)], writes=[("bank", yb)])
                            S.add("dve", lambda e, b=b, n=n, yb=yb: e.tensor_tensor(
                                out=X[:, b, n * 512:(n + 1) * 512], in0=psb[yb][:, :], in1=X[:, b, n * 512:(n + 1) * 512], op=ALU.add),
                                reads=[("bank", yb), XT[b]], writes=[XT[b]])

                ffn1(0)
                if ti + 1 < n_tiles:
                    rms_front(xt[(ti + 1) % 2], NB, [("xt", (ti + 1) % 2, b) for b in range(NB)])
                for g in range(8):
                    if g + 1 < 8:
                        ffn1(g + 1)
                    ffn2(g)

                ckpt(11)
                def final_norm(X=X, XT=XT, t0=t0):
                    for b in range(NB):
                        S.add("act", lambda e, b=b: e.activation(out=junk[:], in_=X[:, b, :], func=AF.Square, accum_out=ssq2[:, b:b + 1]),
                              reads=[XT[b]], writes=["junk", ("ssq2", b)])
                        S.add("act", lambda e, b=b: e.activation(out=rstd2[:, b:b + 1], in_=ssq2[:, b:b + 1], func=AF.Ln,
                                                                  scale=1.0 / D, bias=EPS),
                              reads=[("ssq2", b)], writes=[("rstd2", b)])
                        S.add("act", lambda e, b=b: e.activation(out=rstd2[:, b:b + 1], in_=rstd2[:, b:b + 1], func=AF.Exp, scale=-0.5),
                              reads=[("rstd2", b)], writes=[("rstd2", b)])
                        S.add("dve", lambda e, b=b: e.scalar_tensor_tensor(out=X[:, b, :], in0=X[:, b, :], scalar=rstd2[:, b:b + 1],
                                                                            in1=lnf_rep[:], op0=ALU.mult, op1=ALU.mult),
                              reads=[XT[b], ("rstd2", b), "lnf_rep"], writes=[XT[b]])
                    S.add("pool", lambda e: e.dma_start(out=out_d[t0:t0 + T, :].rearrange("(b p) d -> p b d", p=P), in_=X[:]),
                          reads=XT, dma=True)
                pending_final.append(final_norm)

            while pending_final:
                pending_final.pop(0)()

        except StopBuild:
            pass
        sems = {}
        for name in sorted(S.cnt.keys()):
            sems[name] = es.enter_context(nc.semaphore(name))
        with nc.Block() as block:
            S.emit_all(block, sems)
    return nc


def make_consts():
    ident = np.eye(P, dtype=np.float32)
    tri = np.triu(np.ones((P, P), np.float32))
    ones = np.ones((P, P), np.float32)
    negm = (tri - 1.0) * 30000.0
    return np.ascontiguousarray(np.concatenate([ident, tri, ones, negm], axis=1))


_NC_CACHE = {}


def kernel(**inputs):
    n_cores = 8
    f = lambda a: np.ascontiguousarray(np.asarray(a, dtype=np.float32))
    x = f(inputs["x"])
    B, S_, _ = x.shape
    n_tiles = S_ // T
    if n_tiles not in _NC_CACHE:
        _NC_CACHE[n_tiles] = build_nc(n_tiles)
    nc = _NC_CACHE[n_tiles]
    shared = {
        "ln1": f(inputs["ln1"])[0], "w_in": f(inputs["w_in"])[0], "ml_conv_w": f(inputs["ml_conv_w"])[0],
        "ml_conv_b": f(inputs["ml_conv_b"])[0], "ml_b_i": f(inputs["ml_b_i"])[0], "ml_b_f": f(inputs["ml_b_f"])[0],
        "ml_norm": f(inputs["ml_norm"])[0], "fx_b_f": f(inputs["fx_b_f"])[0], "w_out": f(inputs["w_out"])[0],
        "ln_x": f(inputs["ln_x"])[0], "ln_mem": f(inputs["ln_mem"])[0], "w_xq": f(inputs["w_xq"])[0],
        "w_xkv": f(inputs["w_xkv"])[0], "w_xo": f(inputs["w_xo"])[0], "ln2": f(inputs["ln2"])[0],
        "w_ff1": f(inputs["w_ff1"])[0], "w_ff2": f(inputs["w_ff2"])[0], "ln_f": f(inputs["ln_f"]),
        "cst": make_consts(),
    }
    mem = f(inputs["mem"])
    in_maps = []
    for c in range(n_cores):
        m = dict(shared)
        m["x"] = x[c]
        m["mem"] = mem[c]
        in_maps.append(m)
    res = run_bass_kernel_spmd(nc, in_maps, core_ids=list(range(n_cores)))
    return np.stack([res.results[c]["out"] for c in range(n_cores)], axis=0).astype(np.float32)
```

```python
from contextlib import ExitStack
import numpy as np
import concourse.bass as bass
import concourse.mybir as mybir
from concourse.bass_utils import run_bass_kernel_spmd

F32 = mybir.dt.float32
BF16 = mybir.dt.bfloat16
AF = mybir.ActivationFunctionType
ALU = mybir.AluOpType

ENGS = ["pe", "act", "dve", "pool", "sp"]
N_DMA_SEMS = 8

P = 128
D = 1024
SEQ = 8192
T = 512
NB = T // P
NMEM = 256
DFF = 4096
INC = 3600
EPS = 1e-6
NW = 5
WB_ELEMS = 4160
KCH = 16
LN_ALPHA = float(-0.5 * np.log(128.0))


class Sched:
    def __init__(self):
        self.streams = {e: [] for e in ENGS}
        self.known = {e: {} for e in ENGS}
        self.cnt = {}
        self.opinfo = []
        self.last_w = {}
        self.readers = {}
        self.dma_rr = {e: 0 for e in ENGS}

    def add(self, eng, emit, reads=(), writes=(), dma=False):
        idx = len(self.opinfo)
        bank_toks = [t for t in reads if isinstance(t, tuple) and t[0] == "bank"]
        if bank_toks:
            reads = [t for t in reads if not (isinstance(t, tuple) and t[0] == "bank")]
            writes = list(writes) + bank_toks
        deps = set()
        for t in reads:
            w = self.last_w.get(t)
            if w is not None:
                deps.add(w)
        for t in writes:
            w = self.last_w.get(t)
            if w is not None:
                deps.add(w)
            for r in self.readers.get(t, ()):
                deps.add(r)
        for t in reads:
            self.readers.setdefault(t, []).append(idx)
        for t in writes:
            self.last_w[t] = idx
            self.readers[t] = []
        known = self.known[eng]
        waits = []
        for d in sorted(deps, reverse=True):
            sem, val, clock = self.opinfo[d]
            if known.get(sem, 0) >= val:
                continue
            waits.append((sem, val))
            for k, v in clock.items():
                if known.get(k, 0) < v:
                    known[k] = v
        if dma:
            sem = "dma_%s_%d" % (eng, self.dma_rr[eng])
            self.dma_rr[eng] = (self.dma_rr[eng] + 1) % N_DMA_SEMS
            prev = self.cnt.get(sem, 0)
            if known.get(sem, 0) < prev:
                waits.append((sem, prev))
                known[sem] = prev
            val = prev + 16
            inc = 16
        else:
            sem = eng
            val = self.cnt.get(sem, 0) + 1
            inc = 1
        self.cnt[sem] = val
        clock = dict(known)
        clock[sem] = val
        self.opinfo.append((sem, val, clock))
        self.streams[eng].append((waits, emit, sem, inc))
        return idx

    def emit_all(self, block, sems):
        def run(engname):
            def body(eng):
                for waits, emit, sem, inc in self.streams[engname]:
                    for (s, v) in waits:
                        eng.wait_ge(sems[s], v)
                    inst = emit(eng)
                    inst.then_inc(sems[sem], inc)
                if engname == "sp":
                    for s, v in self.cnt.items():
                        eng.wait_ge(sems[s], v)
            return body
        block.tensor(run("pe"))
        block.scalar(run("act"))
        block.vector(run("dve"))
        block.gpsimd(run("pool"))
        block.sync(run("sp"))


def build_nc(n_tiles=SEQ // T, dbg=False):
    nc = bass.Bass("TRN2", target_bir_lowering=False)
    S_LEN = n_tiles * T

    def din(name, shape):
        return nc.dram_tensor(name, shape, F32, kind="ExternalInput").ap()

    x_d = din("x", [S_LEN, D])
    mem_d = din("mem", [NMEM, D])
    ln1_d = din("ln1", [D])
    w_in_d = din("w_in", [D, INC])
    convw_d = din("ml_conv_w", [4, D])
    convb_d = din("ml_conv_b", [D])
    bi_d = din("ml_b_i", [4])
    bf_d = din("ml_b_f", [4])
    mlnorm_d = din("ml_norm", [512])
    fxbf_d = din("fx_b_f", [8])
    w_out_d = din("w_out", [D, D])
    lnx_d = din("ln_x", [D])
    lnmem_d = din("ln_mem", [D])
    w_xq_d = din("w_xq", [D, D])
    w_xkv_d = din("w_xkv", [D, 2 * D])
    w_xo_d = din("w_xo", [D, D])
    ln2_d = din("ln2", [D])
    w_ff1_d = din("w_ff1", [D, DFF])
    w_ff2_d = din("w_ff2", [DFF, D])
    lnf_d = din("ln_f", [D])
    cst_d = din("cst", [P, 512])
    out_d = nc.dram_tensor("out", [S_LEN, D], F32, kind="ExternalOutput").ap()

    def dscr(name, shape, dt=BF16):
        return nc.dram_tensor(name, shape, dt, kind="Internal").ap()

    wb_in = dscr("wb_in", [D, INC])
    wb_out = dscr("wb_out", [D, D])
    wb_xq = dscr("wb_xq", [D, D])
    wb_xkv = dscr("wb_xkv", [D, 2 * D])
    wb_xo = dscr("wb_xo", [D, D])
    wb_ff1 = dscr("wb_ff1", [D, DFF])
    wb_ff2 = dscr("wb_ff2", [DFF, D])
    kT_d = dscr("kT_d", [4, P, S_LEN])
    v_d = dscr("v_d", [4, S_LEN, 256])

    S = Sched()
    es = ExitStack()
    import os
    STAGE = int(os.environ.get('K_STAGE', '99'))

    class StopBuild(Exception):
        pass

    def ckpt(n):
        if STAGE <= n:
            raise StopBuild()
    with es:
        def sb(name, shape, dt):
            return es.enter_context(nc.sbuf_tensor(name, shape, dt))

        xt = [sb("xt%d" % i, [P, NB, D], F32) for i in range(2)]
        hT = sb("hT", [P, 8, T], BF16)
        tok = sb("tok", [P, NB, D], BF16)
        junk = sb("junk", [P, D], BF16)
        fT2 = sb("fT2", [P, 8, T], BF16)
        stage = [sb("stage%d" % i, [P, T + 3], F32) for i in range(2)]
        acc = [sb("acc%d" % i, [P, T], F32) for i in range(2)]
        halo = sb("halo", [P, 8, 3], F32)
        mv_aug = sb("mv_aug", [P, NB, 4, 129], BF16)
        og = sb("og", [P, NB, 512], F32)
        fqz = sb("fqz", [P, 4, 2, T], BF16)
        kT_cur = sb("kT_cur", [P, 4, T], BF16)
        v_cur = sb("v_cur", [P, NB, 8, P], BF16)
        PT2 = [sb("PT%d" % i, [P, 2, T], BF16) for i in range(3)]
        PT = [PT2[0][:, 0, :], PT2[0][:, 1, :], PT2[1][:, 0, :], PT2[1][:, 1, :]]
        vst = sb("vst", [P, 8, P], BF16)
        Gs = sb("Gs", [P, 2, 8], F32)
        nonlocal_state = {"u": 0}
        osb = [sb("osb%d" % i, [P, T], F32) for i in range(2)]
        bc = [sb("bc%d" % i, [P, T], F32) for i in range(2)]
        uT = [sb("uT%d" % i, [P, 4, T], BF16) for i in range(2)]
        rl = [sb("rl%d" % i, [P, T], BF16) for i in range(2)]
        wbuf = [sb("wbuf%d" % i, [P, WB_ELEMS], BF16) for i in range(NW)]
        memKT = sb("memKT", [P, 8, NMEM], BF16)
        memV = sb("memV", [P, 2, D], BF16)
        cs = sb("cs", [P, 512], F32)
        identb = sb("identb", [P, P], BF16)
        negmb = sb("negmb", [P, P], BF16)
        onesb = sb("onesb", [P, P], BF16)
        lnf_rep = sb("lnf_rep", [P, D], F32)
        mln_rep = sb("mln_rep", [P, 512], F32)
        gcol = sb("gcol", [P, 4, 8], F32)
        cw = sb("cw", [P, 4, 8], F32)
        cb = sb("cb", [P, 8], F32)
        gbias = sb("gbias", [P, 16], F32)
        wg = sb("wg", [P, 8, 16], BF16)
        negc = sb("negc", [P, 8, SEQ // P], F32)
        bias_i = sb("bias_i", [P, 8, SEQ // P], F32)
        spaccs = sb("spaccs", [P, NB + 1, 8], F32)
        crefn = sb("crefn", [P, 8], F32)
        Cst = sb("Cst", [P, 4, 129], F32)
        Cb = sb("Cb", [P, 4, 129], BF16)
        ssq = sb("ssq", [P, NB], F32)
        ssq2 = sb("ssq2", [P, NB], F32)
        rstd2 = sb("rstd2", [P, NB], F32)
        rstd = sb("rstd", [P, NB], F32)
        gs = sb("gs", [P, NB, 16], F32)
        sp_ = sb("sp_", [P, NB, 12], F32)
        gsm = sb("gsm", [P, NB, 24], F32)
        es_ = sb("es_", [P, NB, 4], F32)
        eb_ = sb("eb_", [P, NB, 4], F32)
        eg_ = sb("eg_", [P, NB, 4], F32)
        ktok = sb("ktok", [P, 2, 4, P], BF16)
        Sm = sb("Sm", [P, 2, 4, P], BF16)
        vw = sb("vw", [P, NB, 4, 129], BF16)
        tri4 = sb("tri4", [P, 4, P], F32)
        nd = sb("nd", [P, 4, 129], F32)
        sm1 = sb("sm1", [P, 4, 4], F32)
        t1 = sb("t1", [P, 4, P], BF16)
        memx = xt[1]

        psum = es.enter_context(nc.psum_tensor("psum", [P, 8, 512], F32))
        psb = [psum[:, i, :] for i in range(8)]
        ptrs = [psum[:, 6, :].bitcast(BF16), psum[:, 7, :].bitcast(BF16)]
        psb_b = [psum[:, i, :].bitcast(BF16) for i in range(8)]
        pm = psum[:, 7, :]

        gen_rr = [0]

        def gen_bank():
            b = gen_rr[0]
            gen_rr[0] = (gen_rr[0] + 1) % 4
            return b

        wb_rr = [0]

        def next_wbuf():
            i = wb_rr[0]
            wb_rr[0] = (wb_rr[0] + 1) % NW
            return i

        def load_w(src_ap, shape3, deps=()):
            i = next_wbuf()
            n = shape3[1] * shape3[2]
            view = wbuf[i][:, 0:n].rearrange("p (a b) -> p a b", a=shape3[1])
            S.add("sp", lambda e, v=view, s=src_ap: e.dma_start(out=v, in_=s), reads=list(deps),
                  writes=[("wbuf", i)], dma=True)
            return i, view

        def wpiece(wd, c0, ncols=512):
            return wd.rearrange("(c p) n -> p c n", p=P)[:, :, c0:c0 + ncols]

        try:
            def cast_w(src, dst, rows, name):
                for r0 in range(0, rows, 256):
                    S.add("pool", lambda e, r0=r0: e.dma_start(out=dst[r0:r0 + 256, :], in_=src[r0:r0 + 256, :]),
                          writes=[(name, r0)], dma=True)
            WTOK = {}

            def wtoks(name, rows):
                return [(name, r0) for r0 in range(0, rows, 256)]
            S.add("pool", lambda e: e.memset(halo[:], 0.0), writes=["halo"])
            S.add("pool", lambda e: e.memset(spaccs[:], 0.0), writes=[("spaccs", b) for b in range(NB + 1)])
            S.add("pool", lambda e: e.memset(Cst[:], 0.0), writes=[("Cst", h) for h in range(4)])
            S.add("pool", lambda e: e.memset(Cb[:], 0.0), writes=[("Cb", h) for h in range(4)])
            S.add("pool", lambda e: e.memset(mv_aug[:], 1.0), writes=[("mv_aug", b) for b in range(NB)])
            S.add("pool", lambda e: e.memset(v_cur[:], 0.0), writes=[("v_cur", b) for b in range(NB)])
            v_cur5 = v_cur[:].rearrange("p b (pp hh) c -> p b pp hh c", hh=2)
            S.add("pool", lambda e: e.memset(v_cur5[:, :, :, 0, 64:65], 1.0), writes=[("v_cur", b) for b in range(NB)])
            S.add("pool", lambda e: e.memset(v_cur5[:, :, :, 1, 0:1], 1.0), writes=[("v_cur", b) for b in range(NB)])
            S.add("pool", lambda e: e.memset(fqz[:], 0.0), writes=[("fqz", c) for c in range(4)])

            cast_w(w_xkv_d, wb_xkv, D, "wb_xkv")
            cast_w(w_in_d, wb_in, D, "wb_in")
            cast_w(w_out_d, wb_out, D, "wb_out")
            cast_w(w_xq_d, wb_xq, D, "wb_xq")
            cast_w(w_xo_d, wb_xo, D, "wb_xo")
            cast_w(w_ff1_d, wb_ff1, D, "wb_ff1")
            cast_w(w_ff2_d, wb_ff2, DFF, "wb_ff2")

            def ld(dst, src, tokname, slow=False):
                if slow:
                    S.add("sp", lambda e: e.dma_start(out=dst, in_=src, allow_slow_non_contiguous=True),
                          writes=[tokname], dma=True)
                else:
                    S.add("sp", lambda e: e.dma_start(out=dst, in_=src), writes=[tokname], dma=True)

            ld(cs[:], cst_d, "cs")
            for i, g in enumerate([ln1_d, lnx_d, ln2_d, lnmem_d]):
                ld(gcol[:, i, :], g.rearrange("(c p) -> p c", p=P), ("gcol", i), slow=True)
            for j in range(4):
                ld(cw[:, j, :], convw_d[j, :].rearrange("(c p) -> p c", p=P), ("cw", j), slow=True)
            ld(cb[:], convb_d.rearrange("(c p) -> p c", p=P), "cb", slow=True)
            ld(gbias[:, 0:4], bi_d.partition_broadcast(P), ("gbias", 0))
            ld(gbias[:, 4:8], bf_d.partition_broadcast(P), ("gbias", 1))
            ld(gbias[:, 8:16], fxbf_d.partition_broadcast(P), ("gbias", 2))
            ld(lnf_rep[:], lnf_d.partition_broadcast(P), "lnf_rep")
            ld(mln_rep[:], mlnorm_d.partition_broadcast(P), "mln_rep")
            S.add("sp", lambda e: e.dma_start(out=memx[:, 0:2, :], in_=mem_d.rearrange("(b p) d -> p b d", p=P)), writes=[("xt", 1, 0), ("xt", 1, 1)], dma=True)
            S.add("sp", lambda e: e.dma_start(out=wg[:, :, 0:8], in_=wpiece(wb_in, 2048, 8),
                                             allow_slow_non_contiguous=True),
                  reads=wtoks("wb_in", D), writes=[("wg", 0)], dma=True)
            S.add("sp", lambda e: e.dma_start(out=wg[:, :, 8:16], in_=wpiece(wb_in, 3592, 8),
                                             allow_slow_non_contiguous=True),
                  reads=wtoks("wb_in", D), writes=[("wg", 1)], dma=True)
            W_IN = wtoks("wb_in", D)
            GB = [("gbias", 0), ("gbias", 1), ("gbias", 2)]
            WG = [("wg", 0), ("wg", 1)]

            S.add("dve", lambda e: e.tensor_copy(out=identb[:], in_=cs[:, 0:128]), reads=["cs"], writes=["identb"])
            S.add("dve", lambda e: e.tensor_copy(out=onesb[:], in_=cs[:, 256:384]), reads=["cs"], writes=["onesb"])
            S.add("dve", lambda e: e.tensor_copy(out=negmb[:], in_=cs[:, 384:512]), reads=["cs"], writes=["negmb"])
            tri_f = cs[:, 128:256]
            for h in range(4):
                S.add("dve", lambda e, h=h: e.tensor_copy(out=tri4[:, h, :], in_=cs[:, 128:256]), reads=["cs"], writes=["tri4"])
            ones_f = cs[:, 256:384]
            def rms_to_hT(src, nblk, gi, src_toks, ntok, dstT, dst_tok):
                rms_front(src, nblk, src_toks)
                rms_back(nblk, gi, ntok, dstT, dst_tok)

            def rms_front(src, nblk, src_toks):
                for b in range(nblk):
                    S.add("act", lambda e, b=b: e.activation(out=junk[:], in_=src[:, b, :], func=AF.Square,
                                                              accum_out=ssq[:, b:b + 1]),
                          reads=[src_toks[b]], writes=["junk", ("ssq", b)])
                    S.add("act", lambda e, b=b: e.activation(out=rstd[:, b:b + 1], in_=ssq[:, b:b + 1], func=AF.Ln,
                                                              scale=1.0 / D, bias=EPS),
                          reads=[("ssq", b)], writes=[("rstd", b)])
                    S.add("act", lambda e, b=b: e.activation(out=rstd[:, b:b + 1], in_=rstd[:, b:b + 1], func=AF.Exp, scale=-0.5),
                          reads=[("rstd", b)], writes=[("rstd", b)])
                    S.add("dve", lambda e, b=b: e.tensor_scalar(
                        out=tok[:, b, :], in0=src[:, b, :], scalar1=rstd[:, b:b + 1], scalar2=None, op0=ALU.mult),
                        reads=[src_toks[b], ("rstd", b)], writes=[("tok", b)])
            def rms_back(nblk, gi, ntok, dstT, dst_tok):
                for c in range(8):
                    half = c % 2

                    def tr(e, c=c, half=half):
                        for b in range(nblk):
                            i = e.transpose(out=ptrs[half][:, b * P:(b + 1) * P],
                                            in_=tok[:, b, c * P:(c + 1) * P], identity=identb[:])
                        return i
                    S.add("pe", tr, reads=[("tok", b) for b in range(nblk)] + ["identb"], writes=[("bank", 6 + half)])
                    eng = "act" if c % 2 == 0 else "dve"
                    if eng == "act":
                        S.add("act", lambda e, c=c, half=half: e.activation(
                            out=dstT[:, c, 0:ntok], in_=ptrs[half][:, 0:ntok], func=AF.Copy,
                            scale=gcol[:, gi, c:c + 1]),
                            reads=[("bank", 6 + half), ("gcol", gi)], writes=[(dst_tok, c)])
                    else:
                        S.add("dve", lambda e, c=c, half=half: e.tensor_scalar(
                            out=dstT[:, c, 0:ntok], in0=ptrs[half][:, 0:ntok],
                            scalar1=gcol[:, gi, c:c + 1], scalar2=None, op0=ALU.mult),
                            reads=[("bank", 6 + half), ("gcol", gi)], writes=[(dst_tok, c)])

            def mm_feat(wview, wslot, cc, actT, act_toks, ntok, bank):
                def f(e):
                    for k in range(8):
                        i = e.matmul(psb[bank][:, 0:ntok], lhsT=wview[:, k, cc * P:(cc + 1) * P], rhs=actT[:, k, 0:ntok],
                                     start=(k == 0), stop=(k == 7))
                    return i
                S.add("pe", f, reads=[("wbuf", wslot)] + act_toks, writes=[("bank", bank)])

            def mm_tok(wview, wslot, actT, act_toks, b, bank, ncols=512, c0=0, extra_reads=()):
                def f(e):
                    for k in range(8):
                        i = e.matmul(psb[bank][:, 0:ncols], lhsT=actT[:, k, b * P:(b + 1) * P],
                                     rhs=wview[:, k, c0:c0 + ncols], start=(k == 0), stop=(k == 7))
                    return i
                S.add("pe", f, reads=[("wbuf", wslot)] + act_toks + list(extra_reads), writes=[("bank", bank)])

            HT = [("hT", c) for c in range(8)]

            ckpt(0)
            rms_to_hT(memx, 2, 3, [("xt", 1, 0), ("xt", 1, 1)], NMEM, hT, "hT")
            def load_w_dep(src_ap, shape3, dep_toks):
                i = next_wbuf()
                n = shape3[1] * shape3[2]
                view = wbuf[i][:, 0:n].rearrange("p (a b) -> p a b", a=shape3[1])
                S.add("sp", lambda e, v=view, s=src_ap: e.dma_start(out=v, in_=s), reads=dep_toks,
                      writes=[("wbuf", i)], dma=True)
                return i, view

            for piece in range(2):
                slot, wv = load_w_dep(wpiece(wb_xkv, piece * 512), (P, 8, 512), wtoks("wb_xkv", D))
                for cc in range(4):
                    bank = gen_bank()
                    mm_feat(wv, slot, cc, hT, HT, NMEM, bank)
                    c = piece * 4 + cc
                    S.add("act", lambda e, c=c, bank=bank: e.activation(out=memKT[:, c, :], in_=psb[bank][:, 0:NMEM],
                                                                         func=AF.Copy),
                          reads=[("bank", bank)], writes=[("memKT", c)])
            for piece in range(2):
                slot, wv = load_w_dep(wpiece(wb_xkv, D + piece * 512), (P, 8, 512), wtoks("wb_xkv", D))
                for mb in range(2):
                    bank = gen_bank()
                    mm_tok(wv, slot, hT, HT, mb, bank)
                    S.add("dve", lambda e, mb=mb, piece=piece, bank=bank: e.tensor_copy(
                        out=memV[:, mb, piece * 512:(piece + 1) * 512], in_=psb[bank][:, :]),
                        reads=[("bank", bank)], writes=[("memV", mb, piece)])
            MEMK = [("memKT", c) for c in range(8)]
            MEMV = [("memV", mb, pc) for mb in range(2) for pc in range(2)]

            ckpt(1)
            def load_x(i):
                xb = xt[i % 2]
                S.add("sp", lambda e: e.dma_start(out=xb[:], in_=x_d[i * T:(i + 1) * T, :].rearrange("(b p) d -> p b d", p=P)),
                      writes=[("xt", i % 2, b) for b in range(NB)], dma=True)

            pending_final = []
            load_x(0)
            for ti in range(n_tiles):
                X = xt[ti % 2]
                XT = [("xt", ti % 2, b) for b in range(NB)]
                t0 = ti * T

                if ti == 0:
                    rms_front(X, NB, XT)
                rms_back(NB, 0, T, hT, "hT")
                while pending_final:
                    pending_final.pop(0)()

                ckpt(2)
                for b in range(NB):
                    gb = 4 + b

                    def fg(e, b=b, gb=gb):
                        for k in range(8):
                            i = e.matmul(psb[gb][:, 0:16], lhsT=hT[:, k, b * P:(b + 1) * P],
                                         rhs=wg[:, k, :], start=(k == 0), stop=(k == 7))
                        return i
                    S.add("pe", fg, reads=HT + WG, writes=[("bank", gb)])
                    S.add("dve", lambda e, b=b, gb=gb: e.tensor_tensor(out=gs[:, b, :], in0=psb[gb][:, 0:16],
                                                                        in1=gbias[:], op=ALU.add),
                          reads=[("bank", gb)] + GB, writes=[("gs", b)])
                    S.add("act", lambda e, b=b: e.activation(out=sp_[:, b, :], in_=gs[:, b, 4:16], func=AF.Exp, scale=-1.0),
                          reads=[("gs", b)], writes=[("sp", b)])
                    S.add("act", lambda e, b=b: e.activation(out=sp_[:, b, :], in_=sp_[:, b, :], func=AF.Ln, bias=1.0),
                          reads=[("sp", b)], writes=[("sp", b)])

                for b in range(NB):
                    S.add("dve", lambda e, b=b: e.tensor_tensor(out=spaccs[:, b + 1, :], in0=spaccs[:, b, :], in1=sp_[:, b, 4:12],
                                                                 op=ALU.add),
                          reads=[("spaccs", b), ("sp", b)], writes=[("spaccs", b + 1)])
                for piece in range(2):
                    slot, wv = load_w(wpiece(wb_in, piece * 512), (P, 8, 512), W_IN)
                    for cc in range(4):
                        c = piece * 4 + cc
                        bank = gen_bank()
                        mm_feat(wv, slot, cc, hT, HT, T, bank)
                        st = stage[c % 2]
                        ac = acc[c % 2]
                        stn = ("stage", c % 2)
                        acn = ("acc", c % 2)
                        S.add("pool", lambda e, c=c, st=st: e.tensor_copy(out=st[:, 0:3], in_=halo[:, c, :]),
                              reads=["halo"], writes=[stn])
                        S.add("act", lambda e, st=st, bank=bank: e.activation(out=st[:, 3:T + 3], in_=psb[bank][:, :],
                                                                               func=AF.Copy),
                              reads=[("bank", bank)], writes=[stn])
                        S.add("pool", lambda e, c=c, st=st: e.tensor_copy(out=halo[:, c, :], in_=st[:, T:T + 3]),
                              reads=[stn], writes=["halo"])
                        S.add("act", lambda e, c=c, ac=ac, bank=bank: e.activation(
                            out=ac[:], in_=psb[bank][:, :], func=AF.Identity, scale=cw[:, 3, c:c + 1], bias=cb[:, c:c + 1]),
                            reads=[("bank", bank), ("cw", 3), "cb"], writes=[acn])
                        for j in (2, 1, 0):
                            S.add("dve", lambda e, c=c, st=st, ac=ac, j=j: e.scalar_tensor_tensor(
                                out=ac[:], in0=st[:, j:T + j], scalar=cw[:, j, c:c + 1], in1=ac[:],
                                op0=ALU.mult, op1=ALU.add),
                                reads=[stn, ("cw", j), acn], writes=[acn])
                        S.add("act", lambda e, c=c, ac=ac: e.activation(out=fT2[:, c, :], in_=ac[:], func=AF.Silu),
                              reads=[acn], writes=[("fT2", c)])
                ckpt(4)
                for b in range(NB):
                    gb = 4 + b

                    def fcs(e, b=b, gb=gb):
                        e.matmul(psb[gb][:, 16:28], lhsT=tri_f, rhs=sp_[:, b, :], start=True, stop=True)
                        e.matmul(psb[gb][:, 28:32], lhsT=ones_f, rhs=sp_[:, b, 0:4], start=True, stop=True)
                        e.matmul(psb[gb][:, 32:40], lhsT=tri_f, rhs=sp_[:, b, 4:12], start=True, stop=False)
                        i = e.matmul(psb[gb][:, 32:40], lhsT=ones_f, rhs=spaccs[:, b, :], start=False, stop=True)
                        if b == 1:
                            i = e.matmul(psb[gb][:, 40:48], lhsT=ones_f, rhs=spaccs[:, 2, :], start=True, stop=True)
                        return i
                    S.add("pe", fcs, reads=["cs", ("sp", b), ("spaccs", b)] + ([("spaccs", 2)] if b == 1 else []),
                          writes=[("bank", gb)])
                    S.add("dve", lambda e, b=b, gb=gb: e.tensor_copy(out=gsm[:, b, :], in_=psb[gb][:, 16:40]),
                          reads=[("bank", gb)], writes=[("gsm", b)])
                    if b == 1:
                        S.add("dve", lambda e, gb=gb: e.tensor_copy(out=crefn[:], in_=psb[gb][:, 40:48]),
                              reads=[("bank", gb)], writes=["crefn"])
                    blk = ti * NB + b
                    S.add("dve", lambda e, blk=blk, b=b: e.tensor_copy(out=negc[:, :, blk], in_=gsm[:, b, 16:24]),
                          reads=[("gsm", b)], writes=["negc"])
                    if b % 2 == 1:
                        S.add("dve", lambda e, b=b: e.tensor_tensor(out=Gs[:, b // 2, :], in0=gsm[:, b, 16:24], in1=gsm[:, b - 1, 16:24],
                                                                     op=ALU.subtract),
                              reads=[("gsm", b), ("gsm", b - 1)], writes=[("Gs", b // 2)])
                        S.add("act", lambda e, b=b: e.activation(out=Gs[:, b // 2, :], in_=Gs[:, b // 2, :], func=AF.Exp),
                              reads=[("Gs", b // 2)], writes=[("Gs", b // 2)])
                    S.add("dve", lambda e, b=b: e.tensor_tensor(out=es_[:, b, :], in0=gsm[:, b, 0:4], in1=gs[:, b, 0:4],
                                                                 op=ALU.add),
                          reads=[("gsm", b), ("gs", b)], writes=[("es", b)])
                    S.add("act", lambda e, b=b: e.activation(out=es_[:, b, :], in_=es_[:, b, :], func=AF.Exp,
                                                              bias=LN_ALPHA),
                          reads=[("es", b)], writes=[("es", b)])
                    S.add("act", lambda e, b=b: e.activation(out=eb_[:, b, :], in_=gsm[:, b, 0:4], func=AF.Exp, scale=-1.0),
                          reads=[("gsm", b)], writes=[("eb", b)])
                    S.add("act", lambda e, b=b: e.activation(out=eg_[:, b, :], in_=gsm[:, b, 12:16], func=AF.Exp, scale=-1.0),
                          reads=[("gsm", b)], writes=[("eg", b)])
                nblk_tot = (ti + 1) * NB
                for h in range(8):
                    S.add("dve", lambda e, h=h, nblk_tot=nblk_tot: e.tensor_scalar(out=bias_i[:, h, 0:nblk_tot], in0=negc[:, h, 0:nblk_tot],
                                                                 scalar1=crefn[:, h:h + 1], scalar2=None, op0=ALU.subtract),
                          reads=["negc", "crefn"], writes=[("bias_i", h)])
                S.add("dve", lambda e: e.tensor_copy(out=spaccs[:, 0, :], in_=spaccs[:, NB, :]),
                      reads=[("spaccs", NB)], writes=[("spaccs", 0)])
                slot, wv = load_w(wpiece(wb_in, 1024), (P, 8, 512), W_IN)
                for b in range(NB):
                    bank = gen_bank()
                    mm_tok(wv, slot, hT, HT, b, bank)
                    S.add("dve", lambda e, b=b, bank=bank: e.tensor_copy(
                        out=mv_aug[:, b, :, 0:128], in_=psb[bank][:, :].rearrange("p (h d) -> p h d", h=4)),
                        reads=[("bank", bank)], writes=[("mv_aug", b)])
                slot, wv = load_w(wpiece(wb_in, 1536), (P, 8, 512), W_IN)
                for b in range(NB):
                    bank = gen_bank()
                    mm_tok(wv, slot, hT, HT, b, bank)
                    S.add("act", lambda e, b=b, bank=bank: e.activation(out=og[:, b, :], in_=psb[bank][:, :], func=AF.Sigmoid),
                          reads=[("bank", bank)], writes=[("og", b)])
                    S.add("pool", lambda e, b=b: e.tensor_tensor(out=og[:, b, :], in0=og[:, b, :], in1=mln_rep[:], op=ALU.mult),
                          reads=[("og", b), "mln_rep"], writes=[("og", b)])
                slot, wv = load_w(wpiece(wb_in, 2056), (P, 8, 512), W_IN)
                for cc in range(4):
                    bank = gen_bank()
                    mm_feat(wv, slot, cc, hT, HT, T, bank)
                    S.add("dve", lambda e, cc=cc, bank=bank: e.tensor_copy(out=fqz[0:64, cc, 0, :], in_=psb[bank][0:64, :]),
                          reads=[("bank", bank)], writes=[("fqz", cc)])
                    S.add("dve", lambda e, cc=cc, bank=bank: e.tensor_copy(out=fqz[64:128, cc, 1, :], in_=psb[bank][64:128, :]),
                          reads=[("bank", bank)], writes=[("fqz", cc)])
                slot, wv = load_w(wpiece(wb_in, 2568), (P, 8, 512), W_IN)
                for cc in range(4):
                    bank = gen_bank()
                    mm_feat(wv, slot, cc, hT, HT, T, bank)
                    S.add("act", lambda e, cc=cc, bank=bank: e.activation(out=kT_cur[:, cc, :], in_=psb[bank][:, :], func=AF.Copy),
                          reads=[("bank", bank)], writes=[("kT_cur", cc)])
                    if ti < n_tiles - 1:
                        S.add("pool", lambda e, cc=cc, t0=t0: e.dma_start(out=kT_d[cc, :, t0:t0 + T], in_=kT_cur[:, cc, :]),
                              reads=[("kT_cur", cc)], writes=[("kT_d", cc, ti)], dma=True)
                slot, wv = load_w(wpiece(wb_in, 3080), (P, 8, 512), W_IN)
                for b in range(NB):
                    bank = gen_bank()
                    mm_tok(wv, slot, hT, HT, b, bank)
                    for hh in range(2):
                        S.add("dve", lambda e, b=b, bank=bank, hh=hh: e.tensor_copy(
                            out=v_cur5[:, b, :, hh, hh * 64:hh * 64 + 64],
                            in_=psb[bank][:, :].rearrange("p (pp hh d) -> p pp hh d", pp=4, hh=2)[:, :, hh, :]),
                            reads=[("bank", bank)], writes=[("v_cur", b)])
                    if ti < n_tiles - 1:
                        if b % 2 == 1:
                            for h in range(8):
                                S.add("dve", lambda e, b=b, h=h: e.tensor_scalar(out=vst[:, h, :], in0=v_cur[:, b, h, :],
                                                                                  scalar1=Gs[:, b // 2, h:h + 1], scalar2=None, op0=ALU.mult),
                                      reads=[("v_cur", b), ("Gs", b // 2)], writes=[("vst", h // 2)])
                        for pp in range(4):
                            if b % 2 == 1:
                                S.add("pool", lambda e, b=b, pp=pp, t0=t0: e.dma_start(
                                    out=v_d[pp, t0 + b * P: t0 + (b + 1) * P, :].rearrange("p (h d) -> p h d", h=2),
                                    in_=vst[:, 2 * pp:2 * pp + 2, :]),
                                    reads=[("vst", pp)], writes=[("v_d", pp, ti)], dma=True)
                            else:
                                S.add("pool", lambda e, b=b, pp=pp, t0=t0: e.dma_start(
                                    out=v_d[pp, t0 + b * P: t0 + (b + 1) * P, :].rearrange("p (h d) -> p h d", h=2),
                                    in_=v_cur[:, b, 2 * pp:2 * pp + 2, :]),
                                    reads=[("v_cur", b)], writes=[("v_d", pp, ti)], dma=True)

                ckpt(5)
                if ti + 1 < n_tiles:
                    load_x(ti + 1)

                for b in range(NB):
                    for h in range(4):
                        S.add("pool", lambda e, b=b, h=h: e.tensor_scalar(out=vw[:, b, h, :], in0=mv_aug[:, b, h, :],
                                                                            scalar1=es_[:, b, h:h + 1], scalar2=None, op0=ALU.mult),
                              reads=[("mv_aug", b), ("es", b)], writes=[("vw", b)])
                ml_steps = []
                MB = [6, 7]

                def stageA(b):
                    par = b % 2

                    def sT():
                        ba = MB[0]

                        def ftr(e):
                            for h in range(4):
                                i = e.transpose(out=psb_b[ba][:, h * P:(h + 1) * P], in_=fT2[:, 4 + h, b * P:(b + 1) * P],
                                                identity=identb[:])
                            return i
                        S.add("pe", ftr, reads=[("fT2", 4 + h) for h in range(4)] + ["identb"], writes=[("bank", ba)])
                        S.add("dve", lambda e: e.tensor_copy(out=ktok[:, par, :, :],
                                                             in_=psb_b[ba][:, 0:512].rearrange("p (h d) -> p h d", h=4)),
                              reads=[("bank", ba)], writes=[("ktok", par)])

                    def sS():
                        bs_ = MB[1]

                        def fst(e):
                            for h in range(4):
                                i = e.matmul(psb[bs_][:, h * P:(h + 1) * P], lhsT=fT2[:, 4 + h, b * P:(b + 1) * P],
                                             rhs=fT2[:, h, b * P:(b + 1) * P], start=True, stop=True)
                            return i
                        S.add("pe", fst, reads=[("fT2", c) for c in range(8)], writes=[("bank", bs_)])
                        S.add("dve", lambda e: e.tensor_tensor(out=Sm[:, par, :, :],
                                                               in0=psb[bs_][:, :].rearrange("p (h d) -> p h d", h=4),
                                                               in1=tri4[:], op=ALU.mult),
                              reads=[("bank", bs_), "tri4"], writes=[("Sm", par)])
                    ml_steps.append(sT)
                    ml_steps.append(sS)

                def stageB(b):
                    par = b % 2

                    def sU(half):
                        ub = MB[half]

                        def fu(e):
                            for hq in range(2):
                                h = 2 * half + hq
                                i = e.matmul(psb[ub][:, hq * 129:(hq + 1) * 129], lhsT=ktok[:, par, h, :], rhs=vw[:, b, h, :],
                                             start=True, stop=True)
                            return i
                        S.add("pe", fu, reads=[("ktok", par), ("vw", b)], writes=[("bank", ub)])
                        for hq in range(2):
                            h = 2 * half + hq
                            S.add("dve", lambda e, h=h: e.tensor_scalar(out=Cst[:, h, :], in0=Cst[:, h, :],
                                                                         scalar1=eg_[:, b, h:h + 1], scalar2=None, op0=ALU.mult),
                                  reads=[("Cst", h), ("eg", b)], writes=[("Cst", h)])
                            S.add("dve", lambda e, h=h, hq=hq: e.scalar_tensor_tensor(
                                out=Cst[:, h, :], in0=psb[ub][:, hq * 129:(hq + 1) * 129], scalar=eg_[:, b, h:h + 1], in1=Cst[:, h, :],
                                op0=ALU.mult, op1=ALU.add),
                                reads=[("bank", ub), ("Cst", h), ("eg", b)], writes=[("Cst", h)])

                    def sO(half):
                        ob = MB[half]

                        def fo_(e):
                            for hq in range(2):
                                h = 2 * half + hq
                                e.matmul(psb[ob][:, hq * 129:(hq + 1) * 129], lhsT=Sm[:, par, h, :], rhs=vw[:, b, h, :],
                                         start=True, stop=False)
                                i = e.matmul(psb[ob][:, hq * 129:(hq + 1) * 129], lhsT=fT2[:, h, b * P:(b + 1) * P],
                                             rhs=Cb[:, h, :], start=False, stop=True)
                            return i
                        S.add("pe", fo_, reads=[("Sm", par), ("vw", b), ("Cb", 2 * half), ("Cb", 2 * half + 1)]
                              + [("fT2", 2 * half), ("fT2", 2 * half + 1)], writes=[("bank", ob)])
                        for hq in range(2):
                            h = 2 * half + hq
                            S.add("dve", lambda e, h=h, hq=hq: e.tensor_scalar(out=nd[:, h, :], in0=psb[ob][:, hq * 129:(hq + 1) * 129],
                                                                                scalar1=eb_[:, b, h:h + 1], scalar2=None, op0=ALU.mult),
                                  reads=[("bank", ob), ("eb", b)], writes=[("nd", h)])

                    def sTail():
                        for h in range(4):
                            S.add("pool", lambda e, h=h: e.tensor_copy(out=Cb[:, h, :], in_=Cst[:, h, :]),
                                  reads=[("Cst", h)], writes=[("Cb", h)])
                        NDT = [("nd", h) for h in range(4)]
                        S.add("dve", lambda e: e.tensor_scalar(out=sm1[:, 0, :], in0=nd[:, :, 128], scalar1=-1.0, scalar2=None, op0=ALU.mult),
                              reads=NDT, writes=["sm1"])
                        S.add("dve", lambda e: e.scalar_tensor_tensor(out=sm1[:, 0, :], in0=nd[:, :, 128], scalar=1.0, in1=sm1[:, 0, :],
                                                                      op0=ALU.max, op1=ALU.max),
                              reads=NDT + ["sm1"], writes=["sm1"])
                        S.add("dve", lambda e: e.reciprocal(out=sm1[:, 1, :], in_=sm1[:, 0, :]), reads=["sm1"], writes=["sm1"])
                        for h in range(4):
                            S.add("act", lambda e, h=h: e.activation(out=t1[:, h, :], in_=nd[:, h, 0:128], func=AF.Square,
                                                                      scale=sm1[:, 1, h:h + 1], accum_out=sm1[:, 2, h:h + 1]),
                                  reads=[("nd", h), "sm1"], writes=[("t1", h), ("sm1b", h)])
                        SB4 = [("sm1b", h) for h in range(4)]
                        S.add("act", lambda e: e.activation(out=sm1[:, 3, :], in_=sm1[:, 2, :], func=AF.Ln, scale=1.0 / 128, bias=EPS),
                              reads=SB4, writes=["sm1c"])
                        S.add("act", lambda e: e.activation(out=sm1[:, 3, :], in_=sm1[:, 3, :], func=AF.Exp, scale=-0.5),
                              reads=["sm1c"], writes=["sm1c"])
                        S.add("dve", lambda e: e.tensor_tensor(out=sm1[:, 3, :], in0=sm1[:, 3, :], in1=sm1[:, 1, :], op=ALU.mult),
                              reads=["sm1c", "sm1"], writes=["sm1c"])
                        for h in range(4):
                            S.add("dve", lambda e, h=h: e.scalar_tensor_tensor(out=tok[:, b, h * 128:(h + 1) * 128], in0=nd[:, h, 0:128],
                                                                                scalar=sm1[:, 3, h:h + 1],
                                                                                in1=og[:, b, h * 128:(h + 1) * 128],
                                                                                op0=ALU.mult, op1=ALU.mult),
                                  reads=[("nd", h), "sm1c", ("og", b), ("t1", h)], writes=[("tok", b)])
                    ml_steps.append(lambda: sO(0))
                    ml_steps.append(lambda: sO(1))
                    ml_steps.append(lambda: sU(0))
                    ml_steps.append(lambda: sU(1))
                    ml_steps.append(sTail)

                for b in range(NB):
                    stageA(b)
                    stageB(b)
                for c in range(4):
                    def sMix(c=c):
                        mb_ = MB[c % 2]

                        def trm(e):
                            for b in range(NB):
                                i = e.transpose(out=psb_b[mb_][:, b * P:(b + 1) * P],
                                                in_=tok[:, b, c * P:(c + 1) * P], identity=identb[:])
                            return i
                        S.add("pe", trm, reads=[("tok", b) for b in range(NB)] + ["identb"], writes=[("bank", mb_)])
                        S.add("dve", lambda e: e.tensor_copy(out=hT[:, c, :], in_=psb_b[mb_][:, 0:T]),
                              reads=[("bank", mb_)], writes=[("hT", c)])
                    ml_steps.append(sMix)

                ckpt(7)
                npast = ti * NB
                LOOK = 2
                ml_every = max(1, (32 + 16 * ti) // 36)
                pend = []
                deferred = []

                def tick():
                    for d in deferred:
                        d[0] -= 1
                    while deferred and deferred[0][0] <= 0:
                        deferred.pop(0)[1]()

                def push_unit(s_fn, s_reads, exp_fn, exp_reads, pv_fn):
                    nonlocal_state["u"] += 1
                    u = nonlocal_state["u"]
                    bank = 2 * (u % 2)
                    pt = u % 3
                    S.add("pe", lambda e, bank=bank: s_fn(e, bank), reads=s_reads, writes=[("bank", bank), ("bank", bank + 1)])
                    S.add("act", lambda e, bank=bank, pt=pt: exp_fn(e, bank, pt),
                          reads=[("bank", bank), ("bank", bank + 1)] + exp_reads, writes=[("PT2", pt)])
                    pend.append(lambda pt=pt: pv_fn(pt))
                    if len(pend) > LOOK:
                        pend.pop(0)()
                    tick()
                    if ml_steps and nonlocal_state["u"] % ml_every == 0:
                        ml_steps.pop(0)()

                for pp in range(4):
                    obank = [4, 5]
                    first = [True, True]
                    for c0 in range(0, npast, KCH):
                        nblk = min(KCH, npast - c0)
                        ki = next_wbuf()
                        kview = wbuf[ki][:, 0:nblk * P]
                        kdeps = [("kT_d", pp, tj) for tj in range(c0 // NB, (c0 + nblk) // NB)]
                        S.add("sp", lambda e, kview=kview, pp=pp, c0=c0, nblk=nblk: e.dma_start(
                            out=kview, in_=kT_d[pp, :, c0 * P:(c0 + nblk) * P]),
                            reads=kdeps, writes=[("wbuf", ki)], dma=True)
                        vi = next_wbuf()
                        vview = wbuf[vi][:, 0:nblk * 256].rearrange("p (j d) -> p j d", j=nblk)
                        vdeps = [("v_d", pp, tj) for tj in range(c0 // NB, (c0 + nblk) // NB)]
                        S.add("sp", lambda e, vview=vview, pp=pp, c0=c0, nblk=nblk: e.dma_start(
                            out=vview, in_=v_d[pp, c0 * P:(c0 + nblk) * P, :].rearrange("(j p) d -> p j d", p=P)),
                            reads=vdeps, writes=[("wbuf", vi)], dma=True)
                        for j in range(0, nblk, 2):
                            for hh in range(2):
                                h = 2 * pp + hh

                                def s_fn(e, bank, hh=hh, j=j, kview=kview, pp=pp):
                                    e.matmul(psb[bank][:, :], lhsT=kview[:, j * P:(j + 1) * P], rhs=fqz[:, pp, hh, :],
                                             start=True, stop=True)
                                    return e.matmul(psb[bank + 1][:, :], lhsT=kview[:, (j + 1) * P:(j + 2) * P], rhs=fqz[:, pp, hh, :],
                                                    start=True, stop=True)

                                def exp_fn(e, bank, pt, h=h, jb=c0 + j):
                                    return e.activation(out=PT2[pt][:, :, :], in_=psum[:, bank:bank + 2, :], func=AF.Exp, scale=0.125,
                                                        bias=bias_i[:, h, jb:jb + 1])

                                def pv_fn(pt, hh=hh, j=j, vview=vview, vi=vi, st=first[hh], ob=obank[hh]):
                                    def f(e):
                                        e.matmul(psb[ob][:, :], lhsT=vview[:, j, hh * P:(hh + 1) * P], rhs=PT2[pt][:, 0, :],
                                                 start=st, stop=False)
                                        return e.matmul(psb[ob][:, :], lhsT=vview[:, j + 1, hh * P:(hh + 1) * P], rhs=PT2[pt][:, 1, :],
                                                        start=False, stop=False)
                                    S.add("pe", f, reads=[("wbuf", vi), ("PT2", pt)], writes=[("bank", ob)])
                                push_unit(s_fn, [("wbuf", ki), ("fqz", pp)], exp_fn, [("bias_i", h)], pv_fn)
                                first[hh] = False
                    for jj in range(NB):
                        for hh in range(2):
                            h = 2 * pp + hh
                            pr = slice(hh * 64, hh * 64 + 64)
                            q0 = jj * P

                            def s_fn(e, bank, hh=hh, jj=jj, q0=q0, pp=pp):
                                e.matmul(psb[bank][:, q0:q0 + P], lhsT=identb[:], rhs=negmb[:], start=True, stop=False)
                                i = e.matmul(psb[bank][:, q0:q0 + P], lhsT=kT_cur[:, pp, jj * P:(jj + 1) * P],
                                             rhs=fqz[:, pp, hh, q0:q0 + P], start=False, stop=True)
                                if q0 + P < T:
                                    i = e.matmul(psb[bank][:, q0 + P:T], lhsT=kT_cur[:, pp, jj * P:(jj + 1) * P],
                                                 rhs=fqz[:, pp, hh, q0 + P:T], start=True, stop=True)
                                return i

                            def exp_fn(e, bank, pt, h=h, jb=npast + jj, q0=q0):
                                return e.activation(out=PT2[pt][:, 0, q0:T], in_=psb[bank][:, q0:T], func=AF.Exp, scale=0.125,
                                                    bias=bias_i[:, h, jb:jb + 1])

                            def pv_fn(pt, hh=hh, h=h, jj=jj, q0=q0, st=first[hh], ob=obank[hh], pp=pp):
                                S.add("pe", lambda e: e.matmul(psb[ob][:, q0:T], lhsT=v_cur[:, jj, h, :], rhs=PT2[pt][:, 0, q0:T],
                                                               start=st, stop=(jj == NB - 1)),
                                      reads=[("v_cur", jj), ("PT2", pt)], writes=[("bank", ob)])
                                if jj == NB - 1:
                                    dr = 64 if hh == 0 else 0
                                    rows = slice(0, 64) if hh == 0 else slice(64, 128)
                                    S.add("dve", lambda e: e.tensor_copy(out=osb[hh][:, :], in_=psb[ob][:, :]),
                                          reads=[("bank", ob)], writes=[("osb", hh)])
                                    S.add("dve", lambda e: e.reciprocal(out=osb[hh][dr:dr + 1, :], in_=osb[hh][dr:dr + 1, :]),
                                          reads=[("osb", hh)], writes=[("osb", hh)])

                                    def norm2():
                                        S.add("pe", lambda e: e.matmul(psb[6][:, :], lhsT=cs[dr:dr + 1, 256:384], rhs=osb[hh][dr:dr + 1, :],
                                                                       start=True, stop=True),
                                              reads=[("osb", hh), "cs"], writes=[("bank", 6)])
                                        S.add("dve", lambda e: e.tensor_tensor(out=hT[rows, 4 + pp, :], in0=psb[6][rows, :],
                                                                                in1=osb[hh][rows, :], op=ALU.mult),
                                              reads=[("bank", 6), ("osb", hh)], writes=[("hT", 4 + pp)])
                                    deferred.append([3, norm2])
                            push_unit(s_fn, [("kT_cur", pp), ("fqz", pp), "identb", "negmb"], exp_fn, [("bias_i", h)], pv_fn)
                            first[hh] = False
                while pend:
                    pend.pop(0)()
                    tick()
                while deferred:
                    deferred.pop(0)[1]()
                while ml_steps:
                    ml_steps.pop(0)()

                ckpt(8)
                pieces = []
                for n in range(2):
                    slot, wv = load_w(wpiece(wb_out, n * 512), (P, 8, 512), wtoks('wb_out', D))
                    pieces.append((slot, wv))
                for b in range(NB):
                    for n in range(2):
                        bank = gen_bank()
                        slot, wv = pieces[n]

                        def fo(e, b=b, wv=wv, bank=bank):
                            for k in range(8):
                                i = e.matmul(psb[bank][:, :], lhsT=hT[:, k, b * P:(b + 1) * P], rhs=wv[:, k, :],
                                             start=(k == 0), stop=(k == 7))
                            return i
                        S.add("pe", fo, reads=[("wbuf", slot)] + HT, writes=[("bank", bank)])
                        S.add("dve", lambda e, X=X, b=b, n=n, bank=bank: e.tensor_tensor(
                            out=X[:, b, n * 512:(n + 1) * 512], in0=psb[bank][:, :], in1=X[:, b, n * 512:(n + 1) * 512], op=ALU.add),
                            reads=[("bank", bank), XT[b]], writes=[XT[b]])

                ckpt(9)
                rms_to_hT(X, NB, 1, XT, T, hT, "hT")
                pt_rr = 0
                for piece in range(2):
                    slot, wv = load_w(wpiece(wb_xq, piece * 512), (P, 8, 512), wtoks('wb_xq', D))
                    for cc in range(4):
                        c = piece * 4 + cc
                        bank = gen_bank()
                        mm_feat(wv, slot, cc, hT, HT, T, bank)
                        S.add("act" if cc % 2 == 0 else "dve",
                              (lambda e, c=c, bank=bank: e.activation(out=fT2[:, c, :], in_=psb[bank][:, :], func=AF.Copy))
                              if cc % 2 == 0 else
                              (lambda e, c=c, bank=bank: e.tensor_copy(out=fT2[:, c, :], in_=psb[bank][:, :])),
                              reads=[("bank", bank)], writes=[("fT2", c)])
                for grp in range(2):
                    heads = [2 * grp, 2 * grp + 1]
                    sbanks = {}
                    for hq, h in enumerate(heads):
                        for mb in range(2):
                            bank = gen_bank()
                            sbanks[(hq, mb)] = bank

                            def fxs(e, h=h, mb=mb, bank=bank):
                                for cc in range(2):
                                    i = e.matmul(psb[bank][:, :], lhsT=memKT[:, 2 * h + cc, mb * P:(mb + 1) * P],
                                                 rhs=fT2[:, 2 * h + cc, :], start=(cc == 0), stop=(cc == 1))
                                return i
                            S.add("pe", fxs, reads=MEMK + [("fT2", 2 * h), ("fT2", 2 * h + 1)], writes=[("bank", bank)])
                    for hq, h in enumerate(heads):
                        for mb in range(2):
                            bank = sbanks[(hq, mb)]
                            pt = 2 * hq + mb
                            S.add("act", lambda e, bank=bank, pt=pt: e.activation(out=PT[pt], in_=psb[bank][:, :], func=AF.Exp,
                                                                                   scale=1.0 / 16),
                                  reads=[("bank", bank)], writes=[("PT2", pt // 2)])
                    for hq, h in enumerate(heads):
                        db = 6 + hq

                        def fden(e, hq=hq, db=db):
                            for mb in range(2):
                                i = e.matmul(psb[db][:, :], lhsT=onesb[:], rhs=PT[2 * hq + mb], start=(mb == 0), stop=(mb == 1))
                            return i
                        S.add("pe", fden, reads=["onesb", ("PT2", hq)], writes=[("bank", db)])
                    pvb = {}
                    for hq, h in enumerate(heads):
                        for cc in range(2):
                            ob = (4 + cc) if hq == 0 else gen_bank()
                            pvb[(hq, cc)] = ob

                            def fpv(e, h=h, hq=hq, cc=cc, ob=ob):
                                for mb in range(2):
                                    i = e.matmul(psb[ob][:, :], lhsT=memV[:, mb, (2 * h + cc) * P:(2 * h + cc + 1) * P],
                                                 rhs=PT[2 * hq + mb], start=(mb == 0), stop=(mb == 1))
                                return i
                            S.add("pe", fpv, reads=MEMV + [("PT2", hq)], writes=[("bank", ob)])
                    for hq, h in enumerate(heads):
                        db = 6 + hq
                        S.add("act", lambda e, hq=hq, db=db: e.activation(out=bc[hq][:], in_=psb[db][:, :], func=AF.Ln),
                              reads=[("bank", db)], writes=[("bc", hq)])
                        S.add("act", lambda e, hq=hq: e.activation(out=bc[hq][:], in_=bc[hq][:], func=AF.Exp, scale=-1.0),
                              reads=[("bc", hq)], writes=[("bc", hq)])
                    for hq, h in enumerate(heads):
                        for cc in range(2):
                            ob = pvb[(hq, cc)]
                            S.add("dve", lambda e, h=h, hq=hq, cc=cc, ob=ob: e.tensor_tensor(out=hT[:, 2 * h + cc, :], in0=psb[ob][:, :],
                                                                                              in1=bc[hq][:], op=ALU.mult),
                                  reads=[("bank", ob), ("bc", hq)], writes=[("hT", 2 * h + cc)])
                for n in range(2):
                    slot, wv = load_w(wpiece(wb_xo, n * 512), (P, 8, 512), wtoks('wb_xo', D))
                    for b in range(NB):
                        bank = gen_bank()
                        mm_tok(wv, slot, hT, HT, b, bank)
                        S.add("dve", lambda e, X=X, b=b, n=n, bank=bank: e.tensor_tensor(
                            out=X[:, b, n * 512:(n + 1) * 512], in0=psb[bank][:, :], in1=X[:, b, n * 512:(n + 1) * 512], op=ALU.add),
                            reads=[("bank", bank), XT[b]], writes=[XT[b]])

                ckpt(10)
                rms_to_hT(X, NB, 2, XT, T, hT, "hT")
                ybank_rr = 0
                ffn_w2 = {}

                def ffn1(g):
                    s1, w1 = load_w(wpiece(wb_ff1, g * 512), (P, 8, 512), wtoks('wb_ff1', D))
                    i2 = next_wbuf()
                    w2 = wbuf[i2][:, 0:4096].rearrange("p (a b) -> p a b", a=4)
                    S.add("sp", lambda e: e.dma_start(
                        out=w2, in_=wb_ff2[g * 512:(g + 1) * 512, :].rearrange("(c p) n -> p c n", p=P)),
                        reads=wtoks("wb_ff2", DFF), writes=[("wbuf", i2)], dma=True)
                    ffn_w2[g] = (i2, w2)
                    u = uT[g % 2]
                    for cc in range(4):
                        bank = gen_bank()
                        mm_feat(w1, s1, cc, hT, HT, T, bank)
                        r = rl[cc % 2]
                        S.add("act", lambda e, r=r, bank=bank: e.activation(out=r[:], in_=psb[bank][:, :], func=AF.Relu),
                              reads=[("bank", bank)], writes=[("rl", cc % 2)])
                        S.add("pool", lambda e, r=r, u=u, cc=cc: e.tensor_tensor(out=u[:, cc, :], in0=r[:], in1=r[:], op=ALU.mult),
                              reads=[("rl", cc % 2)], writes=[("uT", g % 2, cc)])

                def ffn2(g, X=X):
                    nonlocal_state["yb"] = nonlocal_state.get("yb", 0)
                    i2, w2 = ffn_w2[g]
                    u = uT[g % 2]
                    for b in range(NB):
                        for n in range(2):
                            yb = 4 + nonlocal_state["yb"]
                            nonlocal_state["yb"] = (nonlocal_state["yb"] + 1) % 2

                            def fy(e, b=b, n=n, yb=yb):
                                for cc in range(4):
                                    i = e.matmul(psb[yb][:, :], lhsT=u[:, cc, b * P:(b + 1) * P], rhs=w2[:, cc, n * 512:(n + 1) * 512],
                                                 start=(cc == 0), stop=(cc == 3))
                                return i
                            S.add("pe", fy, reads=[("wbuf", i2)] + [("uT", g % 2, cc) for cc in range(4)], writes=[("bank", yb)])
                            S.add("dve", lambda e, b=b, n=n, yb=yb: e.tensor_tensor(
                                out=X[:, b, n * 512:(n + 1) * 512], in0=psb[yb][:, :], in1=X[:, b, n * 512:(n + 1) * 512], op=ALU.add),
                                reads=[("bank", yb), XT[b]], writes=[XT[b]])

                ffn1(0)
                if ti + 1 < n_tiles:
                    rms_front(xt[(ti + 1) % 2], NB, [("xt", (ti + 1) % 2, b) for b in range(NB)])
                for g in range(8):
                    if g + 1 < 8:
                        ffn1(g + 1)
                    ffn2(g)

                ckpt(11)
                def final_norm(X=X, XT=XT, t0=t0):
                    for b in range(NB):
                        S.add("act", lambda e, b=b: e.activation(out=junk[:], in_=X[:, b, :], func=AF.Square, accum_out=ssq2[:, b:b + 1]),
                              reads=[XT[b]], writes=["junk", ("ssq2", b)])
                        S.add("act", lambda e, b=b: e.activation(out=rstd2[:, b:b + 1], in_=ssq2[:, b:b + 1], func=AF.Ln,
                                                                  scale=1.0 / D, bias=EPS),
                              reads=[("ssq2", b)], writes=[("rstd2", b)])
                        S.add("act", lambda e, b=b: e.activation(out=rstd2[:, b:b + 1], in_=rstd2[:, b:b + 1], func=AF.Exp, scale=-0.5),
                              reads=[("rstd2", b)], writes=[("rstd2", b)])
                        S.add("dve", lambda e, b=b: e.scalar_tensor_tensor(out=X[:, b, :], in0=X[:, b, :], scalar=rstd2[:, b:b + 1],
                                                                            in1=lnf_rep[:], op0=ALU.mult, op1=ALU.mult),
                              reads=[XT[b], ("rstd2", b), "lnf_rep"], writes=[XT[b]])
                    S.add("pool", lambda e: e.dma_start(out=out_d[t0:t0 + T, :].rearrange("(b p) d -> p b d", p=P), in_=X[:]),
                          reads=XT, dma=True)
                pending_final.append(final_norm)

            while pending_final:
                pending_final.pop(0)()

        except StopBuild:
            pass
        sems = {}
        for name in sorted(S.cnt.keys()):
            sems[name] = es.enter_context(nc.semaphore(name))
        with nc.Block() as block:
            S.emit_all(block, sems)
    return nc


def make_consts():
    ident = np.eye(P, dtype=np.float32)
    tri = np.triu(np.ones((P, P), np.float32))
    ones = np.ones((P, P), np.float32)
    negm = (tri - 1.0) * 30000.0
    return np.ascontiguousarray(np.concatenate([ident, tri, ones, negm], axis=1))


_NC_CACHE = {}


def kernel(**inputs):
    n_cores = 8
    f = lambda a: np.ascontiguousarray(np.asarray(a, dtype=np.float32))
    x = f(inputs["x"])
    B, S_, _ = x.shape
    n_tiles = S_ // T
    if n_tiles not in _NC_CACHE:
        _NC_CACHE[n_tiles] = build_nc(n_tiles)
    nc = _NC_CACHE[n_tiles]
    shared = {
        "ln1": f(inputs["ln1"])[0], "w_in": f(inputs["w_in"])[0], "ml_conv_w": f(inputs["ml_conv_w"])[0],
        "ml_conv_b": f(inputs["ml_conv_b"])[0], "ml_b_i": f(inputs["ml_b_i"])[0], "ml_b_f": f(inputs["ml_b_f"])[0],
        "ml_norm": f(inputs["ml_norm"])[0], "fx_b_f": f(inputs["fx_b_f"])[0], "w_out": f(inputs["w_out"])[0],
        "ln_x": f(inputs["ln_x"])[0], "ln_mem": f(inputs["ln_mem"])[0], "w_xq": f(inputs["w_xq"])[0],
        "w_xkv": f(inputs["w_xkv"])[0], "w_xo": f(inputs["w_xo"])[0], "ln2": f(inputs["ln2"])[0],
        "w_ff1": f(inputs["w_ff1"])[0], "w_ff2": f(inputs["w_ff2"])[0], "ln_f": f(inputs["ln_f"]),
        "cst": make_consts(),
    }
    mem = f(inputs["mem"])
    in_maps = []
    for c in range(n_cores):
        m = dict(shared)
        m["x"] = x[c]
        m["mem"] = mem[c]
        in_maps.append(m)
    res = run_bass_kernel_spmd(nc, in_maps, core_ids=list(range(n_cores)))
    return np.stack([res.results[c]["out"] for c in range(n_cores)], axis=0).astype(np.float32)
```
